# Optimizing a Trainium2 kernel written in Bass

```python
import math
import jax, jax.numpy as jnp
from jax import lax
import numpy as np

D_MODEL = 2048
BATCH = 8
SEQ = 2048
DEPTH = 2

EPS = 1e-6
ROPE_THETA = 10000.0
Q_BLOCK = 128
NEG_INF = -1e30

MLA_HEADS = D_MODEL // 128
MLA_Q_LORA = D_MODEL // 4
MLA_KV_LORA = D_MODEL // 4
MLA_NOPE = 128
MLA_ROPE = 64
MLA_V = 128
MLA_DOWN = MLA_Q_LORA + MLA_KV_LORA + MLA_ROPE

DIL_PAIRS = ((128, 1), (512, 4), (2048, 16))
DIL_GROUPS = len(DIL_PAIRS)
DIL_HEADS = D_MODEL // 128
DIL_HEAD_DIM = 128

FFN_HIDDEN = ((8 * D_MODEL + 3 * 256 - 1) // (3 * 256)) * 256

kernel_name = "hybrid_mla_dilated_swiglu"


def rms_norm(x, gain):
    xf = x.astype(jnp.float32)
    y = xf * lax.rsqrt(jnp.mean(xf * xf, axis=-1, keepdims=True) + EPS)
    return (y * gain.astype(jnp.float32)).astype(x.dtype)


def apply_rope(x, positions):
    dim = x.shape[-1]
    half = dim // 2
    inv_freq = jnp.power(ROPE_THETA, -2.0 * jnp.arange(half, dtype=jnp.float32) / dim)
    ang = positions.astype(jnp.float32)[..., None] * inv_freq
    ang = ang.reshape(ang.shape[:2] + (1,) * (x.ndim - 3) + (half,))
    cos, sin = jnp.cos(ang), jnp.sin(ang)
    xf = x.astype(jnp.float32)
    x1, x2 = xf[..., :half], xf[..., half:]
    return jnp.concatenate([x1 * cos - x2 * sin, x2 * cos + x1 * sin], axis=-1).astype(x.dtype)


def causal_block_attention(q, k, v, scale):
    B, S, H, Dk = q.shape
    Dv = v.shape[-1]
    nb = S // Q_BLOCK
    qb = q.reshape(B, nb, Q_BLOCK, H, Dk).transpose(1, 0, 2, 3, 4)
    kpos = jnp.arange(S)

    def one_block(args):
        qblk, start = args
        s = jnp.einsum('bqhd,bkhd->bhqk', qblk, k, preferred_element_type=jnp.float32) * scale
        qpos = start + jnp.arange(Q_BLOCK)
        mask = kpos[None, :] <= qpos[:, None]
        s = jnp.where(mask[None, None], s, NEG_INF)
        p = jax.nn.softmax(s, axis=-1)
        return jnp.einsum('bhqk,bkhd->bqhd', p.astype(v.dtype), v)

    o = lax.map(one_block, (qb, jnp.arange(nb) * Q_BLOCK))
    return o.transpose(1, 0, 2, 3, 4).reshape(B, S, H, Dv)


def strided_window_attention(q, k, v, dilation, span, scale):
    B, S, H, Dh = q.shape
    L = S // dilation
    N = B * dilation

    def to_residue(t):
        return t.reshape(B, L, dilation, H, Dh).transpose(0, 2, 1, 3, 4).reshape(N, L, H, Dh)

    qr, kr, vr = to_residue(q), to_residue(k), to_residue(v)
    blk = math.gcd(L, Q_BLOCK)
    nb = L // blk
    slab = span + blk
    pad = ((0, 0), (span, 0), (0, 0), (0, 0))
    kp, vp = jnp.pad(kr, pad), jnp.pad(vr, pad)
    idx = jnp.arange(nb)[:, None] * blk + jnp.arange(slab)[None, :]
    kb, vb = kp[:, idx], vp[:, idx]
    qb = qr.reshape(N, nb, blk, H, Dh)
    s = jnp.einsum('nbqhd,nbkhd->nbhqk', qb, kb, preferred_element_type=jnp.float32) * scale
    i = jnp.arange(blk)[:, None]
    j = jnp.arange(slab)[None, :]
    dist = i + span - j
    key_pos = jnp.arange(nb)[:, None, None] * blk + j[None] - span
    valid = (dist >= 0) & (dist <= span) & (key_pos >= 0)
    s = jnp.where(valid[None, :, None], s, NEG_INF)
    m = jnp.max(s, axis=-1, keepdims=True)
    p = jnp.exp(s - m)
    l = jnp.sum(p, axis=-1, keepdims=True)
    o = jnp.einsum('nbhqk,nbkhd->nbqhd', (p / l).astype(v.dtype), vb)
    lse = (m + jnp.log(l))[..., 0]
    o = o.reshape(B, dilation, L, H, Dh).transpose(0, 2, 1, 3, 4).reshape(B, S, H, Dh)
    lse = lse.transpose(0, 1, 3, 2).reshape(B, dilation, L, H).transpose(0, 2, 1, 3).reshape(B, S, H)
    return o, lse


def mla_mixer(h, positions, w_down, q_norm, kv_norm, w_uq, w_ukv, q_gain, k_gain, w_o):
    B, S, _ = h.shape
    down = jnp.einsum('bsd,de->bse', h, w_down)
    c_q = rms_norm(down[..., :MLA_Q_LORA], q_norm)
    c_kv = rms_norm(down[..., MLA_Q_LORA:MLA_Q_LORA + MLA_KV_LORA], kv_norm)
    k_rope_raw = down[..., MLA_Q_LORA + MLA_KV_LORA:]
    q = jnp.einsum('bsr,re->bse', c_q, w_uq).reshape(B, S, MLA_HEADS, MLA_NOPE + MLA_ROPE)
    kv = jnp.einsum('bsr,re->bse', c_kv, w_ukv).reshape(B, S, MLA_HEADS, MLA_NOPE + MLA_V)
    q_nope = rms_norm(q[..., :MLA_NOPE], q_gain[:MLA_NOPE])
    q_rope = apply_rope(rms_norm(q[..., MLA_NOPE:], q_gain[MLA_NOPE:]), positions)
    k_nope = rms_norm(kv[..., :MLA_NOPE], k_gain[:MLA_NOPE])
    k_rope = apply_rope(rms_norm(k_rope_raw, k_gain[MLA_NOPE:]), positions)
    v = kv[..., MLA_NOPE:]
    q = jnp.concatenate([q_nope, q_rope], axis=-1)
    k = jnp.concatenate([k_nope, jnp.broadcast_to(k_rope[:, :, None, :], (B, S, MLA_HEADS, MLA_ROPE))], axis=-1)
    o = causal_block_attention(q, k, v, 1.0 / math.sqrt(MLA_NOPE + MLA_ROPE))
    return jnp.einsum('bse,ed->bsd', o.reshape(B, S, MLA_HEADS * MLA_V), w_o)


def dilated_mixer(h, positions, w_qkv, q_gain, k_gain, w_o):
    B, S, _ = h.shape
    qkv = jnp.einsum('bsd,de->bse', h, w_qkv).reshape(B, S, 3, DIL_GROUPS, DIL_HEADS, DIL_HEAD_DIM)
    q = apply_rope(rms_norm(qkv[:, :, 0], q_gain[:, None, :]), positions)
    k = apply_rope(rms_norm(qkv[:, :, 1], k_gain[:, None, :]), positions)
    v = qkv[:, :, 2]
    scale = 1.0 / math.sqrt(DIL_HEAD_DIM)
    outs, lses = [], []
    for g, (window, dilation) in enumerate(DIL_PAIRS):
        o_g, lse_g = strided_window_attention(q[:, :, g], k[:, :, g], v[:, :, g], dilation, window // dilation, scale)
        outs.append(o_g)
        lses.append(lse_g)
    wts = jax.nn.softmax(jnp.stack(lses, axis=0), axis=0)
    o = jnp.sum(wts[..., None] * jnp.stack(outs, axis=0).astype(jnp.float32), axis=0).astype(h.dtype)
    return jnp.einsum('bse,ed->bsd', o.reshape(B, S, DIL_HEADS * DIL_HEAD_DIM), w_o)


def swiglu(h, w_gate, w_up, w_down):
    a = jax.nn.silu(jnp.einsum('bsd,df->bsf', h, w_gate)) * jnp.einsum('bsd,df->bsf', h, w_up)
    return jnp.einsum('bsf,fd->bsd', a, w_down)


def setup_inputs(seed: int = 0) -> dict:
    key = jax.random.key(seed)
    ks = jax.random.split(key, 24)
    n_a = (DEPTH + 1) // 2
    n_b = DEPTH // 2
    f32 = jnp.float32

    def nrm(k, shape, fan_in):
        return jax.random.normal(k, shape, f32) * (fan_in ** -0.5)

    def gain(k, shape):
        return 1.0 + 0.05 * jax.random.normal(k, shape, f32)

    x = jax.random.normal(ks[0], (BATCH, SEQ, D_MODEL), f32)
    offsets = jax.random.randint(ks[1], (BATCH, 1), 0, 4096, dtype=jnp.int32)
    positions = offsets + jnp.arange(SEQ, dtype=jnp.int32)[None, :]
    return {
        "x": x,
        "positions": positions,
        "mixer_norm": gain(ks[2], (DEPTH, D_MODEL)),
        "ffn_norm": gain(ks[3], (DEPTH, D_MODEL)),
        "mla_w_down": nrm(ks[4], (n_a, D_MODEL, MLA_DOWN), D_MODEL),
        "mla_q_norm": gain(ks[5], (n_a, MLA_Q_LORA)),
        "mla_kv_norm": gain(ks[6], (n_a, MLA_KV_LORA)),
        "mla_w_uq": nrm(ks[7], (n_a, MLA_Q_LORA, MLA_HEADS * (MLA_NOPE + MLA_ROPE)), MLA_Q_LORA),
        "mla_w_ukv": nrm(ks[8], (n_a, MLA_KV_LORA, MLA_HEADS * (MLA_NOPE + MLA_V)), MLA_KV_LORA),
        "mla_q_gain": gain(ks[9], (n_a, MLA_NOPE + MLA_ROPE)),
        "mla_k_gain": gain(ks[10], (n_a, MLA_NOPE + MLA_ROPE)),
        "mla_w_o": nrm(ks[11], (n_a, MLA_HEADS * MLA_V, D_MODEL), MLA_HEADS * MLA_V),
        "dil_w_qkv": nrm(ks[12], (n_b, D_MODEL, 3 * DIL_GROUPS * DIL_HEADS * DIL_HEAD_DIM), D_MODEL),
        "dil_q_gain": gain(ks[13], (n_b, DIL_GROUPS, DIL_HEAD_DIM)),
        "dil_k_gain": gain(ks[14], (n_b, DIL_GROUPS, DIL_HEAD_DIM)),
        "dil_w_o": nrm(ks[15], (n_b, DIL_HEADS * DIL_HEAD_DIM, D_MODEL), DIL_HEADS * DIL_HEAD_DIM),
        "ffn_w_gate": nrm(ks[16], (DEPTH, D_MODEL, FFN_HIDDEN), D_MODEL),
        "ffn_w_up": nrm(ks[17], (DEPTH, D_MODEL, FFN_HIDDEN), D_MODEL),
        "ffn_w_down": nrm(ks[18], (DEPTH, FFN_HIDDEN, D_MODEL), FFN_HIDDEN),
    }


def reference(x, positions, mixer_norm, ffn_norm, mla_w_down, mla_q_norm, mla_kv_norm, mla_w_uq,
              mla_w_ukv, mla_q_gain, mla_k_gain, mla_w_o, dil_w_qkv, dil_q_gain, dil_k_gain, dil_w_o,
              ffn_w_gate, ffn_w_up, ffn_w_down):
    for i in range(DEPTH):
        h = rms_norm(x, mixer_norm[i])
        j = i // 2
        if i % 2 == 0:
            mix = mla_mixer(h, positions, mla_w_down[j], mla_q_norm[j], mla_kv_norm[j], mla_w_uq[j],
                            mla_w_ukv[j], mla_q_gain[j], mla_k_gain[j], mla_w_o[j])
        else:
            mix = dilated_mixer(h, positions, dil_w_qkv[j], dil_q_gain[j], dil_k_gain[j], dil_w_o[j])
        x = x + mix
        x = x + swiglu(rms_norm(x, ffn_norm[i]), ffn_w_gate[i], ffn_w_up[i], ffn_w_down[i])
    return x
```

```python
import math
from contextlib import ExitStack
import numpy as np
import ml_dtypes
import concourse.bass as bass
import concourse.mybir as mybir
from concourse.bass_utils import run_bass_kernel_spmd

F32 = mybir.dt.float32
BF16 = mybir.dt.bfloat16
I32 = mybir.dt.int32
AF = mybir.ActivationFunctionType
ALU = mybir.AluOpType

SAME_ENGINE_SYNC = True
LOOKAHEAD = 3
ST_BANKS = (2, 3, 0, 1)
NOSYNC = ()
D = 2048
S = 2048
FF = 5632
EPS = 1e-6
PI = math.pi
TWO_PI = 2.0 * math.pi
C1 = 6.28125
C2 = TWO_PI - C1


class Sem:
    def __init__(self, nc, name):
        self.h = nc.alloc_semaphore(name)
        self.count = 0


class Res:
    __slots__ = ("last_w", "readers")

    def __init__(self):
        self.last_w = None
        self.readers = []


class Eng:
    def __init__(self, nc, name, eng):
        self.name = name
        self.eng = eng
        self.sem = Sem(nc, "s_" + name)
        self.seen = {}
        self.is_pe = name == "pe"


class FW:
    def __init__(self, nc):
        self.nc = nc
        self.pe = Eng(nc, "pe", nc.tensor)
        self.act = Eng(nc, "act", nc.scalar)
        self.dve = Eng(nc, "dve", nc.vector)
        self.pool = Eng(nc, "pool", nc.gpsimd)
        self.sp = Eng(nc, "sp", nc.sync)
        self.engs = [self.pe, self.act, self.dve, self.pool, self.sp]
        self.dma_sems = {}
        self.n_inst = 0
        self._uid = 0

    def uid(self):
        self._uid += 1
        return self._uid

    def dsem(self, name):
        if name not in self.dma_sems:
            self.dma_sems[name] = Sem(self.nc, "d_" + name)
        return self.dma_sems[name]

    def _waits(self, E, reads, writes):
        deps = {}
        for r in reads:
            if r.last_w is not None:
                s, v = r.last_w
                if deps.get(s, 0) < v:
                    deps[s] = v
        for w in writes:
            if w.last_w is not None:
                s, v = w.last_w
                if deps.get(s, 0) < v:
                    deps[s] = v
            for (s, v) in w.readers:
                if deps.get(s, 0) < v:
                    deps[s] = v
        for s, v in deps.items():
            if s is E.sem and (E.is_pe or not SAME_ENGINE_SYNC or E.name in NOSYNC):
                continue
            if E.seen.get(s, 0) >= v:
                continue
            E.eng.wait_ge(s.h, v)
            self.n_inst += 1
            E.seen[s] = v

    def _commit(self, tag, reads, writes):
        for r in reads:
            r.readers.append(tag)
        for w in writes:
            w.last_w = tag
            w.readers = []

    def op(self, E, fn, reads=(), writes=()):
        self._waits(E, reads, writes)
        inst = fn()
        E.sem.count += 1
        inst.then_inc(E.sem.h, 1)
        self.n_inst += 1
        tag = (E.sem, E.sem.count)
        self._commit(tag, reads, writes)
        return tag

    def dma(self, E, sem, pairs, reads=(), writes=()):
        self._waits(E, reads, writes)
        for (o, i) in pairs:
            E.eng.dma_start(out=o, in_=i).then_inc(sem.h, 16)
            sem.count += 16
            self.n_inst += 1
        tag = (sem, sem.count)
        self._commit(tag, reads, writes)
        return tag

    def barrier(self):
        sems = [e.sem for e in self.engs] + list(self.dma_sems.values())
        for E in self.engs:
            for s in sems:
                if s is E.sem:
                    continue
                if s.count > 0 and E.seen.get(s, 0) < s.count:
                    E.eng.wait_ge(s.h, s.count)
                    E.seen[s] = s.count
                    self.n_inst += 1


class Ring:
    def __init__(self, K, stack, name, n, shape, dtype, dma=False):
        self.t = [stack.enter_context(K.nc.sbuf_tensor("%s%d_%d" % (name, i, K.fw.uid()), shape, dtype))
                  for i in range(n)]
        self.r = [Res() for _ in range(n)]
        self.s = [K.fw.dsem("%s%d" % (name, i)) for i in range(n)] if dma else [None] * n
        self.i = -1
        self.n = n

    def next(self):
        self.i = (self.i + 1) % self.n
        return self.t[self.i], self.r[self.i], self.s[self.i]


class Stop(Exception):
    pass


class Prog:
    def ck(self, name):
        if self.stop == name:
            self.fw.barrier()
            self.stopped = True
            return True
        return False

    def __init__(self, debug=False, stop=None, lite=False, skip=()):
        self.debug = debug
        self.stop = stop
        self.stopped = False
        nc = self.nc = bass.Bass("TRN2", target_bir_lowering=False)
        self.fw = FW(nc)
        fw = self.fw

        self.skip = skip

        def din(name, shape, dt=F32):
            if lite and name not in ("x", "pos", "mixer_norm", "ffn_norm", "cols", "cb") + tuple(lite):
                shape = [2, 2]
            return nc.dram_tensor(name, shape, dt, kind="ExternalInput").ap()

        self.x = din("x", [S, D])
        self.pos = din("pos", [1, S], I32)
        self.mixer_norm = din("mixer_norm", [2, D])
        self.ffn_norm = din("ffn_norm", [2, D])
        self.w_down = din("mla_w_down", [D, 1088])
        self.w_uq = din("mla_w_uq", [512, 3072])
        self.w_ukv = din("mla_w_ukv", [512, 4096])
        self.mla_w_o = din("mla_w_o", [D, D])
        self.w_qkv = din("dil_w_qkv_r", [D, 16 * 9 * 128])
        self.dil_w_o = din("dil_w_o", [D, D])
        self.w_gate = din("ffn_w_gate", [2, D, FF])
        self.w_up = din("ffn_w_up", [2, D, FF])
        self.w_dn = din("ffn_w_down", [2, FF, D])
        self.cols_d = din("cols", [128, 32])
        self.cb_d = din("cb", [128, 768], BF16)
        kind = "ExternalOutput" if debug else "Internal"
        self.x1 = nc.dram_tensor("x1", [S, D], F32, kind=kind).ap()
        self.x2 = nc.dram_tensor("x2", [S, D], F32, kind=kind).ap()
        self.x3 = nc.dram_tensor("x3", [S, D], F32, kind=kind).ap()
        self.out = nc.dram_tensor("out", [S, D], F32, kind="ExternalOutput").ap()
        self.ot_scr = nc.dram_tensor("ot_scr", [16, 128, S], BF16, kind=kind).ap()

        self.pb = [nc.alloc_psum_tensor("pb%d" % i, [128, 1024], BF16) for i in range(8)]
        self.pf = [b.bitcast(F32) for b in self.pb]
        self.pr = [Res() for _ in range(8)]

        with ExitStack() as st:
            self.cols = st.enter_context(nc.sbuf_tensor("cols_sb", [128, 32], F32))
            self.cb = st.enter_context(nc.sbuf_tensor("cb_sb", [128, 768], BF16))
            self.cst = st.enter_context(nc.sbuf_tensor("cst", [128, 4], F32))
            self.c_res = Res()
            fw.dma(fw.sp, fw.dsem("c0"), [(self.cols[:], self.cols_d)], writes=[self.c_res])
            fw.dma(fw.sp, fw.dsem("c1"), [(self.cb[:], self.cb_d)], writes=[self.c_res])
            fw.op(fw.dve, lambda: nc.vector.memset(self.cst[:, 0:1], EPS), writes=[self.c_res])
            fw.op(fw.dve, lambda: nc.vector.memset(self.cst[:, 1:2], 0.0), writes=[self.c_res])
            self.ident = self.cb[:, 0:128]
            self.ones = self.cb[:, 128:256]
            self.perm128 = self.cb[:, 256:384]
            self.perm64 = self.cb[0:64, 384:448]
            self.mask_cur = self.cb[:, 512:640]
            self.mask2 = self.cb[:, 512:768]
            self.eps_col = self.cst[:, 0:1]
            self.zero_col = self.cst[:, 1:2]
            fw.barrier()
            try:
                self.body()
            except Stop:
                pass
            fw.barrier()

    def body(self):
        stop = self.stop
        self.mla_mixer(self.x, self.x1)
        if stop == "mla" or self.stopped:
            return
        self.ffn(0, self.x1, self.x2)
        if stop == "ffn0":
            return
        self.dil_mixer(self.x2, self.x3)
        if stop == "dil" or self.stopped:
            return
        self.ffn(1, self.x3, self.out)

    def norm_to_hT(self, st_outer, x_dram, gain_row, hT, hT_res, t0, ntile):
        nc, fw = self.nc, self.fw
        with ExitStack() as st:
            gain = st.enter_context(nc.sbuf_tensor("gain_%d" % fw.uid(), [128, D], F32))
            g_res = Res()
            fw.dma(fw.sp, fw.dsem("gain"), [(gain[:], gain_row.partition_broadcast(128))], writes=[g_res])
            xt_ring = Ring(self, st, "xt", 3, [128, D], F32, dma=True)
            xn_ring = Ring(self, st, "xn", 2, [128, D], BF16)
            ss_ring = Ring(self, st, "nss", 2, [128, 4], F32)
            for ti in range(ntile):
                tok = t0 + ti * 128
                xt, xt_r, xt_s = xt_ring.next()
                fw.dma(fw.sp, xt_s, [(xt[:], x_dram[tok:tok + 128, :])], writes=[xt_r])
                ss, ss_r, _ = ss_ring.next()
                xn, xn_r, _ = xn_ring.next()
                fw.op(fw.act, lambda: nc.scalar.activation(xn[:], xt[:], AF.Square, accum_out=ss[:, 0:1]),
                      reads=[xt_r], writes=[xn_r, ss_r])
                fw.op(fw.act, lambda: nc.scalar.activation(ss[:, 1:2], ss[:, 0:1], AF.Ln, bias=self.eps_col,
                                                           scale=1.0 / D),
                      reads=[ss_r, self.c_res], writes=[ss_r])
                fw.op(fw.act, lambda: nc.scalar.activation(ss[:, 2:3], ss[:, 1:2], AF.Exp, bias=self.zero_col,
                                                           scale=-0.5),
                      reads=[ss_r], writes=[ss_r])
                fw.op(fw.dve, lambda: nc.vector.scalar_tensor_tensor(xn[:], xt[:], ss[:, 2:3], gain[:],
                                                                     ALU.mult, ALU.mult),
                      reads=[xt_r, ss_r, g_res], writes=[xn_r])
                for half in range(2):
                    bank = half + (6 if ti % 2 else 0)
                    pbk = self.pb[bank]

                    def tr():
                        ins = None
                        for j in range(8):
                            k = half * 8 + j
                            ins = nc.tensor.transpose(pbk[:, j * 128:(j + 1) * 128], xn[:, k * 128:(k + 1) * 128],
                                                      self.ident)
                        return ins
                    fw.op(fw.pe, tr, reads=[xn_r, self.c_res], writes=[self.pr[bank]])
                    src = pbk[:].rearrange("p (k t) -> p k t", k=8)
                    dst = hT[:, half * 8:(half + 1) * 8, ti * 128:(ti + 1) * 128]
                    if half == 0:
                        fw.op(fw.act, lambda: nc.scalar.copy(dst, src), reads=[self.pr[bank]],
                              writes=[hT_res[ti][half]])
                    else:
                        fw.op(fw.dve, lambda: nc.vector.tensor_copy(dst, src), reads=[self.pr[bank]],
                              writes=[hT_res[ti][half]])
            fw.barrier()

    def proj_residual(self, aT, aT_reads, nk, ntile, w_dram, res_dram, out_dram, t0, ncol_blk, wname):
        nc, fw = self.nc, self.fw
        cw = D // ncol_blk
        with ExitStack() as st:
            nsplit = 4 if nk >= 16 else 1
            kk = nk // nsplit
            w_rings = [Ring(self, st, "%sq%d" % (wname, j), 2, [128, kk, cw], BF16, dma=True) for j in range(nsplit)]
            xr_ring = Ring(self, st, "xr", 3, [128, cw], F32, dma=True)
            o_ring = Ring(self, st, "ost", 3, [128, cw], F32, dma=True)
            wv = w_dram.rearrange("(k p) d -> p k d", p=128)
            nb = 0
            for cb_i in range(ncol_blk):
                c0 = cb_i * cw
                parts = []
                for j in range(nsplit):
                    wt, wt_r, wt_s = w_rings[j].next()
                    fw.dma(fw.pool, wt_s, [(wt[:], wv[:, j * kk:(j + 1) * kk, c0:c0 + cw])], writes=[wt_r])
                    parts.append((wt, wt_r))
                for ti in range(ntile):
                    tok = t0 + ti * 128
                    bank = 2 + (nb % 2)
                    nb += 1
                    pf = self.pf[bank]
                    for j in range(nsplit):
                        wt, wt_r = parts[j]

                        def mm():
                            ins = None
                            for kq in range(kk):
                                k = j * kk + kq
                                ins = nc.tensor.matmul(pf[:, 0:cw], aT[:, k, ti * 128:(ti + 1) * 128], wt[:, kq, :],
                                                       start=(k == 0), stop=(k == nk - 1))
                            return ins
                        fw.op(fw.pe, mm, reads=[wt_r] + aT_reads(ti), writes=[self.pr[bank]])
                    xr, xr_r, xr_s = xr_ring.next()
                    fw.dma(fw.sp, xr_s, [(xr[:], res_dram[tok:tok + 128, c0:c0 + cw])], writes=[xr_r])
                    ot, ot_r, ot_s = o_ring.next()
                    fw.op(fw.dve, lambda: nc.vector.tensor_tensor(ot[:], pf[:, 0:cw], xr[:], ALU.add),
                          reads=[self.pr[bank], xr_r], writes=[ot_r])
                    fw.dma(fw.sp, ot_s, [(out_dram[tok:tok + 128, c0:c0 + cw], ot[:])], reads=[ot_r])
            fw.barrier()

    def ffn(self, layer, x_in, x_out):
        nc, fw = self.nc, self.fw
        NT = 1024
        for half in range(2):
            t0 = half * NT
            with ExitStack() as st:
                aT = st.enter_context(nc.sbuf_tensor("aT_%d" % fw.uid(), [128, 44, NT], BF16))
                aT_res = [[Res() for _ in range(2)] for _ in range(44)]
                with ExitStack() as st2:
                    hT = st2.enter_context(nc.sbuf_tensor("hTf_%d" % fw.uid(), [128, 16, NT], BF16))
                    hT_res = [[Res(), Res()] for _ in range(NT // 128)]
                    self.norm_to_hT(st2, x_in, self.ffn_norm[layer:layer + 1, :], hT, hT_res, t0, NT // 128)
                    w_ring = Ring(self, st2, "wgu", 3, [128, 2, 16, 256], BF16, dma=True)
                    sg_ring = Ring(self, st2, "sg", 2, [128, 512], F32)
                    wg_v = self.w_gate[layer].rearrange("(k p) f -> p k f", p=128)
                    wu_v = self.w_up[layer].rearrange("(k p) f -> p k f", p=128)
                    nb = 0
                    for fc2 in range(22):
                        wt, wt_r, wt_s = w_ring.next()
                        f0 = fc2 * 256
                        fw.dma(fw.pool, wt_s, [(wt[:, 0], wg_v[:, :, f0:f0 + 256]),
                                               (wt[:, 1], wu_v[:, :, f0:f0 + 256])], writes=[wt_r])
                        for sub in range(2):
                            fc = fc2 * 2 + sub
                            for tb in range(NT // 512):
                                hr = [r for ti in range(tb * 4, tb * 4 + 4) for r in hT_res[ti]]
                                banks = [2 + (nb % 2) * 2, 3 + (nb % 2) * 2]
                                nb += 1
                                for gu in range(2):
                                    pf = self.pf[banks[gu]]

                                    def mm():
                                        ins = None
                                        for k in range(16):
                                            ins = nc.tensor.matmul(pf[:, :], wt[:, gu, k, sub * 128:(sub + 1) * 128],
                                                                   hT[:, k, tb * 512:(tb + 1) * 512],
                                                                   start=(k == 0), stop=(k == 15))
                                        return ins
                                    fw.op(fw.pe, mm, reads=[wt_r] + hr, writes=[self.pr[banks[gu]]])
                                sg, sg_r, _ = sg_ring.next()
                                fw.op(fw.act, lambda: nc.scalar.activation(sg[:], self.pf[banks[0]][:, :], AF.Silu),
                                      reads=[self.pr[banks[0]]], writes=[sg_r])
                                fw.op(fw.dve, lambda: nc.vector.tensor_tensor(aT[:, fc, tb * 512:(tb + 1) * 512],
                                                                              self.pf[banks[1]][:, :], sg[:],
                                                                              ALU.mult),
                                      reads=[self.pr[banks[1]], sg_r], writes=[aT_res[fc][tb]])
                    fw.barrier()
                allr = [r for fr in aT_res for r in fr]
                self.proj_residual(aT, lambda ti: allr, 44, NT // 128, self.w_dn[layer], x_in, x_out, t0, 4, "wdn")

    def rope_tables(self, Dh, invf_col, sgn_col, cosT, sinS, tab_res):
        nc, fw = self.nc, self.fw
        with ExitStack() as st:
            posi = st.enter_context(nc.sbuf_tensor("posi_%d" % fw.uid(), [128, S], I32))
            posf = st.enter_context(nc.sbuf_tensor("posf_%d" % fw.uid(), [128, S], F32))
            ang = st.enter_context(nc.sbuf_tensor("ang_%d" % fw.uid(), [128, 512], F32))
            v = st.enter_context(nc.sbuf_tensor("v_%d" % fw.uid(), [128, 512], F32))
            ki = st.enter_context(nc.sbuf_tensor("ki_%d" % fw.uid(), [128, 512], I32))
            kf = st.enter_context(nc.sbuf_tensor("kf_%d" % fw.uid(), [128, 512], F32))
            m = st.enter_context(nc.sbuf_tensor("m_%d" % fw.uid(), [128, 512], F32))
            R = Res()
            fw.dma(fw.sp, fw.dsem("posi"), [(posi[:], self.pos.partition_broadcast(128))], writes=[R])
            V = nc.vector

            def dv(fn):
                fw.op(fw.dve, fn, reads=[R, self.c_res], writes=[R])
            if "conv0" in self.skip:
                dv(lambda: V.memset(posf[:], 3.0))
            else:
                dv(lambda: V.tensor_copy(posf[:], posi[:]))
            P = slice(0, Dh)
            for tb in range(4 if "short" not in self.skip else 1):
                cs = slice(tb * 512, (tb + 1) * 512)
                for which in range(2):
                    dv(lambda: V.tensor_scalar(ang[P], posf[P, cs], invf_col, None, ALU.mult))
                    if which == 1:
                        dv(lambda: V.tensor_scalar(ang[P], ang[P], PI / 2, None, ALU.add))
                    dv(lambda: V.tensor_scalar(v[P], ang[P], 1.0 / TWO_PI, None, ALU.mult))
                    if "conv" in self.skip:
                        dv(lambda: V.tensor_copy(kf[P], v[P]))
                    else:
                        dv(lambda: V.tensor_copy(ki[P], v[P]))
                        dv(lambda: V.tensor_copy(kf[P], ki[P]))
                    dv(lambda: V.scalar_tensor_tensor(ang[P], kf[P], -C1, ang[P], ALU.mult, ALU.add))
                    dv(lambda: V.scalar_tensor_tensor(ang[P], kf[P], -C2, ang[P], ALU.mult, ALU.add))
                    if "cmp" not in self.skip:
                        dv(lambda: V.tensor_scalar(m[P], ang[P], PI, -TWO_PI, ALU.is_gt, ALU.mult))
                        dv(lambda: V.tensor_tensor(ang[P], ang[P], m[P], ALU.add))
                        dv(lambda: V.tensor_scalar(m[P], ang[P], -PI, TWO_PI, ALU.is_lt, ALU.mult))
                        dv(lambda: V.tensor_tensor(ang[P], ang[P], m[P], ALU.add))
                    dv(lambda: V.tensor_scalar(ang[P], ang[P], 3.1415925, -3.1415925, ALU.min, ALU.max))
                    fw.op(fw.act, lambda: nc.scalar.activation(v[P], ang[P], AF.Sin if "sin" not in self.skip else AF.Copy,
                                                               bias=self.zero_col[P], scale=1.0),
                          reads=[R, self.c_res], writes=[R])
                    if which == 0:
                        dv(lambda: V.tensor_scalar(sinS[P, cs], v[P], sgn_col, None, ALU.mult))
                    else:
                        dv(lambda: V.tensor_copy(cosT[P, cs], v[P]))
            fw.op(fw.dve, lambda: V.tensor_copy(m[P, 0:1], m[P, 0:1]), reads=[R], writes=[R, tab_res])
            fw.barrier()

    def make_post(self, st):
        self.sq_ring = Ring(self, st, "sq", 2, [128, 512], BF16)
        self.qg_ring = Ring(self, st, "qg", 2, [128, 512], BF16)
        self.ln_ring = Ring(self, st, "lnv", 2, [128, 512], F32)
        self.rs_ring = Ring(self, st, "rstd", 2, [128, 512], F32)
        self.t1_ring = Ring(self, st, "t1", 2, [128, 512], F32)
        self.t2_ring = Ring(self, st, "t2", 2, [128, 512], F32)

    def rstd_from_ss(self, Dh, ss_bank, nfeat):
        nc, fw = self.nc, self.fw
        P = slice(0, Dh)
        ln, ln_r, _ = self.ln_ring.next()
        fw.op(fw.act, lambda: nc.scalar.activation(ln[P], self.pf[ss_bank][P, :], AF.Ln, bias=self.eps_col[P],
                                                   scale=1.0 / nfeat),
              reads=[self.pr[ss_bank], self.c_res], writes=[ln_r])
        rs, rs_r, _ = self.rs_ring.next()
        fw.op(fw.act, lambda: nc.scalar.activation(rs[P], ln[P], AF.Exp, bias=self.zero_col[P], scale=-0.5),
              reads=[ln_r, self.c_res], writes=[rs_r])
        return rs, rs_r

    def qk_post(self, Dh, raw_bank, gain_col, out_ap, out_res, rope=None, ss_bank=6, rot_bank=7):
        nc, fw = self.nc, self.fw
        P = slice(0, Dh)
        raw = self.pf[raw_bank]
        rr = self.pr[raw_bank]
        self.flush_post()
        sq, sq_r, _ = self.sq_ring.next()
        fw.op(fw.act, lambda: nc.scalar.activation(sq[P], raw[P, :], AF.Square), reads=[rr], writes=[sq_r])
        qg, qg_r, _ = self.qg_ring.next()
        fw.op(fw.dve, lambda: nc.vector.tensor_scalar(qg[P], raw[P, :], gain_col, None, ALU.mult),
              reads=[rr, self.c_res], writes=[qg_r, rr])
        self.pending_post = lambda: self._post_b(Dh, sq, sq_r, qg, qg_r, out_ap, out_res, rope, ss_bank, rot_bank)

    def flush_post(self):
        prev = getattr(self, "pending_post", None)
        self.pending_post = None
        if prev is not None:
            prev()

    def _post_b(self, Dh, sq, sq_r, qg, qg_r, out_ap, out_res, rope, ss_bank, rot_bank):
        nc, fw = self.nc, self.fw
        P = slice(0, Dh)
        fw.op(fw.pe, lambda: nc.tensor.matmul(self.pf[ss_bank][P, :], self.ones[P, 0:Dh], sq[P],
                                              start=True, stop=True),
              reads=[sq_r, self.c_res], writes=[self.pr[ss_bank]])
        if rope is not None:
            cos_ap, sin_ap, perm_ap, tab_res = rope
            fw.op(fw.pe, lambda: nc.tensor.matmul(self.pf[rot_bank][P, :], perm_ap, qg[P], start=True, stop=True),
                  reads=[qg_r, self.c_res], writes=[self.pr[rot_bank]])
        rs, rs_r = self.rstd_from_ss(Dh, ss_bank, Dh)
        if rope is None:
            fw.op(fw.dve, lambda: nc.vector.tensor_tensor(out_ap, qg[P], rs[P], ALU.mult),
                  reads=[qg_r, rs_r], writes=[out_res])
        else:
            t1, t1_r, _ = self.t1_ring.next()
            t2, t2_r, _ = self.t2_ring.next()
            fw.op(fw.dve, lambda: nc.vector.tensor_tensor(t1[P], qg[P], cos_ap, ALU.mult),
                  reads=[qg_r, tab_res], writes=[t1_r])
            fw.op(fw.dve, lambda: nc.vector.tensor_tensor(t2[P], self.pf[rot_bank][P, :], sin_ap, ALU.mult),
                  reads=[self.pr[rot_bank], tab_res], writes=[t2_r])
            fw.op(fw.dve, lambda: nc.vector.tensor_tensor(t1[P], t1[P], t2[P], ALU.add),
                  reads=[t1_r, t2_r], writes=[t1_r])
            fw.op(fw.dve, lambda: nc.vector.tensor_tensor(out_ap, t1[P], rs[P], ALU.mult),
                  reads=[t1_r, rs_r], writes=[out_res])

    def wo_phase(self, w_o, x_in, x_out):
        nc, fw = self.nc, self.fw
        with ExitStack() as st:
            OT = st.enter_context(nc.sbuf_tensor("OT_%d" % fw.uid(), [128, 16, S], BF16))
            R = Res()
            fw.dma(fw.sp, fw.dsem("otl"), [(OT[:, h, :], self.ot_scr[h]) for h in range(16)], writes=[R])
            self.proj_residual(OT, lambda ti: [R], 16, 16, w_o, x_in, x_out, 0, 4, "wo")

    def mla_mixer(self, x_in, x_out):
        nc, fw = self.nc, self.fw
        cols = self.cols
        with ExitStack() as st:
            cqT = st.enter_context(nc.sbuf_tensor("cqT", [128, 4, S], BF16))
            ckvT = st.enter_context(nc.sbuf_tensor("ckvT", [128, 4, S], BF16))
            krT = st.enter_context(nc.sbuf_tensor("krT", [64, S], BF16))
            cos64 = st.enter_context(nc.sbuf_tensor("cos64", [64, S], BF16))
            sin64 = st.enter_context(nc.sbuf_tensor("sin64", [64, S], BF16))
            tab_res = Res()
            cq_res = [Res() for _ in range(4)]
            ckv_res = [Res() for _ in range(4)]
            kr_res = [Res() for _ in range(4)]
            self.rope_tables(64, cols[0:64, 20:21], cols[0:64, 21:22], cos64, sin64, tab_res)
            if self.ck("tables"):
                return
            with ExitStack() as st1:
                hT = st1.enter_context(nc.sbuf_tensor("hT_m", [128, 16, S], BF16))
                hT_res = [[Res(), Res()] for _ in range(16)]
                wdn = st1.enter_context(nc.sbuf_tensor("wdn_m", [128, 16, 1088], BF16))
                wdn_r = Res()
                wv = self.w_down.rearrange("(k p) e -> p k e", p=128)
                fw.dma(fw.pool, fw.dsem("wdnm"), [(wdn[:, j * 4:(j + 1) * 4, :], wv[:, j * 4:(j + 1) * 4, :])
                                                  for j in range(4)], writes=[wdn_r])
                self.norm_to_hT(st1, x_in, self.mixer_norm[0:1, :], hT, hT_res, 0, 16)
                if self.ck("norm"):
                    return
                self.make_post(st1)
                cg_ring = Ring(self, st1, "cg", 2, [128, 4, 512], BF16)
                nb = 0
                for tb in range(4):
                    hr = [r for ti in range(tb * 4, tb * 4 + 4) for r in hT_res[ti]]
                    ts = slice(tb * 512, (tb + 1) * 512)
                    for grp in range(2):
                        cg, cg_r, _ = cg_ring.next()
                        ss_bank = 4 + grp
                        for c in range(4):
                            oc = grp * 4 + c
                            bank = nb % 3
                            nb += 1
                            pf = self.pf[bank]

                            def mm():
                                ins = None
                                for k in range(16):
                                    ins = nc.tensor.matmul(pf[:, :], wdn[:, k, oc * 128:(oc + 1) * 128], hT[:, k, ts],
                                                           start=(k == 0), stop=(k == 15))
                                return ins
                            fw.op(fw.pe, mm, reads=[wdn_r] + hr, writes=[self.pr[bank]])
                            sq, sq_r, _ = self.sq_ring.next()
                            fw.op(fw.act, lambda: nc.scalar.activation(sq[:], pf[:, :], AF.Square),
                                  reads=[self.pr[bank]], writes=[sq_r])
                            fw.op(fw.dve, lambda: nc.vector.tensor_scalar(cg[:, c, :], pf[:, :], cols[:, oc:oc + 1],
                                                                          None, ALU.mult),
                                  reads=[self.pr[bank], self.c_res], writes=[cg_r, self.pr[bank]])
                            fw.op(fw.pe, lambda: nc.tensor.matmul(self.pf[ss_bank][:, :], self.ones, sq[:],
                                                                  start=(c == 0), stop=(c == 3)),
                                  reads=[sq_r, self.c_res], writes=[self.pr[ss_bank]])
                        rs, rs_r = self.rstd_from_ss(128, ss_bank, 512)
                        dstT = cqT if grp == 0 else ckvT
                        dres = cq_res if grp == 0 else ckv_res
                        for c in range(4):
                            fw.op(fw.dve, lambda: nc.vector.tensor_tensor(dstT[:, c, ts], cg[:, c, :], rs[:],
                                                                          ALU.mult),
                                  reads=[cg_r, rs_r], writes=[dres[tb]])
                    bank = nb % 3
                    nb += 1
                    pf = self.pf[bank]

                    def mmr():
                        ins = None
                        for k in range(16):
                            ins = nc.tensor.matmul(pf[0:64, :], wdn[:, k, 1024:1088], hT[:, k, ts],
                                                   start=(k == 0), stop=(k == 15))
                        return ins
                    fw.op(fw.pe, mmr, reads=[wdn_r] + hr, writes=[self.pr[bank]])
                    self.qk_post(64, bank, cols[0:64, 11:12], krT[:, ts], kr_res[tb],
                                 rope=(cos64[:, ts], sin64[:, ts], self.perm64, tab_res))
                    self.flush_post()
                self.flush_post()
                fw.barrier()
            if self.ck("m1"):
                return
            scale = 1.0 / math.sqrt(192.0)
            with ExitStack() as st2:
                self.make_post(st2)
                wq_ring = Ring(self, st2, "wq", 2, [128, 4, 192], BF16, dma=True)
                wkv_ring = Ring(self, st2, "wkv", 2, [128, 4, 256], BF16, dma=True)
                qn_ring = Ring(self, st2, "qn", 2, [128, S], BF16)
                qr_ring = Ring(self, st2, "qr", 2, [64, S], BF16)
                kn_ring = Ring(self, st2, "kn", 2, [128, S], BF16)
                v_ring = Ring(self, st2, "vh", 2, [128, 16, 128], BF16)
                pt_ring = Ring(self, st2, "pt", LOOKAHEAD + 2, [128, 512], BF16)
                rl_ring = Ring(self, st2, "rl", 2, [128, 512], F32)
                oh_ring = Ring(self, st2, "oh", 2, [128, S], BF16, dma=True)
                wq_v = self.w_uq.rearrange("(k p) e -> p k e", p=128)
                wkv_v = self.w_ukv.rearrange("(k p) e -> p k e", p=128)
                allcq = cq_res
                allckv = ckv_res
                nst = 0
                for h in range(16):
                    wq, wq_r, wq_s = wq_ring.next()
                    fw.dma(fw.pool, wq_s, [(wq[:], wq_v[:, :, h * 192:(h + 1) * 192])], writes=[wq_r])
                    wkv, wkv_r, wkv_s = wkv_ring.next()
                    fw.dma(fw.pool, wkv_s, [(wkv[:], wkv_v[:, :, h * 256:(h + 1) * 256])], writes=[wkv_r])
                    qn, qn_r, _ = qn_ring.next()
                    qr, qr_r, _ = qr_ring.next()
                    kn, kn_r, _ = kn_ring.next()
                    vh, vh_r, _ = v_ring.next()
                    nb = 0
                    for tb in range(4):
                        ts = slice(tb * 512, (tb + 1) * 512)
                        for what in range(3):
                            bank = nb % 2
                            nb += 1
                            pf = self.pf[bank]
                            if what == 0:
                                Dh, wsl, src, srcr = 128, wq[:, :, 0:128], cqT, allcq[tb]
                            elif what == 1:
                                Dh, wsl, src, srcr = 64, wq[:, :, 128:192], cqT, allcq[tb]
                            else:
                                Dh, wsl, src, srcr = 128, wkv[:, :, 0:128], ckvT, allckv[tb]

                            def mm():
                                ins = None
                                for k in range(4):
                                    ins = nc.tensor.matmul(pf[0:Dh, :], wsl[:, k, :], src[:, k, ts],
                                                           start=(k == 0), stop=(k == 3))
                                return ins
                            fw.op(fw.pe, mm, reads=[wq_r if what < 2 else wkv_r, srcr], writes=[self.pr[bank]])
                            if what == 0:
                                self.qk_post(128, bank, cols[:, 8:9], qn[:, ts], qn_r)
                            elif what == 1:
                                self.qk_post(64, bank, cols[0:64, 9:10], qr[:, ts], qr_r,
                                             rope=(cos64[:, ts], sin64[:, ts], self.perm64, tab_res))
                            else:
                                self.qk_post(128, bank, cols[:, 10:11], kn[:, ts], kn_r)
                    self.flush_post()
                    for tg in range(4):
                        bank = nb % 2
                        nb += 1
                        pf = self.pf[bank]

                        def mmv():
                            ins = None
                            for j in range(4):
                                tt = tg * 4 + j
                                for k in range(4):
                                    ins = nc.tensor.matmul(pf[:, j * 128:(j + 1) * 128],
                                                           ckvT[:, k, tt * 128:(tt + 1) * 128], wkv[:, k, 128:256],
                                                           start=(j == 0 and k == 0), stop=(k == 3),
                                                           skip_group_check=True)
                            return ins
                        fw.op(fw.pe, mmv, reads=[wkv_r, allckv[tg]], writes=[self.pr[bank]])
                        fw.op(fw.act, lambda: nc.scalar.copy(vh[:, tg * 4:(tg + 1) * 4, :],
                                                             pf[:, :].rearrange("p (j d) -> p j d", j=4)),
                              reads=[self.pr[bank]], writes=[vh_r])
                    oh, oh_r, oh_s = oh_ring.next()
                    for qb in range(4):
                        qs0 = qb * 512
                        ob, lb = 4, 5
                        nk = 4 * qb + 4
                        pend_pv = []
                        for j in range(nk):
                            r = j - 4 * qb
                            c0 = 128 * r if r > 0 else 0
                            sb = ST_BANKS[nst % len(ST_BANKS)]
                            nst += 1
                            ks = slice(j * 128, (j + 1) * 128)
                            qs = slice(qs0 + c0, qs0 + 512)

                            def mms():
                                nc.tensor.matmul(self.pf[sb][:, c0:512], kn[:, ks], qn[:, qs], start=True, stop=False)
                                return nc.tensor.matmul(self.pf[sb][:, c0:512], krT[:, ks], qr[:, qs],
                                                        start=False, stop=True)
                            fw.op(fw.pe, mms, reads=[kn_r, qn_r, qr_r] + kr_res, writes=[self.pr[sb]])
                            pt, pt_r, _ = pt_ring.next()
                            fw.op(fw.act, lambda: nc.scalar.activation(pt[:, c0:512], self.pf[sb][:, c0:512], AF.Exp,
                                                                       bias=self.zero_col, scale=scale),
                                  reads=[self.pr[sb], self.c_res], writes=[pt_r])
                            if r >= 0:
                                fw.op(fw.dve, lambda: nc.vector.tensor_tensor(pt[:, c0:c0 + 128], pt[:, c0:c0 + 128],
                                                                              self.mask_cur, ALU.mult),
                                      reads=[pt_r, self.c_res], writes=[pt_r])

                            def mk_pv(j=j, c0=c0, pt=pt, pt_r=pt_r):
                                def mmo():
                                    nc.tensor.matmul(self.pf[ob][:, c0:512], vh[:, j, :], pt[:, c0:512],
                                                     start=(j == 0), stop=(j == nk - 1), skip_group_check=True)
                                    return nc.tensor.matmul(self.pf[lb][:, c0:512], self.ones, pt[:, c0:512],
                                                            start=(j == 0), stop=(j == nk - 1),
                                                            skip_group_check=True)
                                return lambda: fw.op(fw.pe, mmo, reads=[vh_r, pt_r, self.c_res],
                                                     writes=[self.pr[ob], self.pr[lb]])
                            pend_pv.append(mk_pv())
                            while len(pend_pv) > LOOKAHEAD:
                                pend_pv.pop(0)()
                        while pend_pv:
                            pend_pv.pop(0)()
                        rl, rl_r, _ = rl_ring.next()
                        fw.op(fw.dve, lambda: nc.vector.reciprocal(rl[:], self.pf[lb][:, :]), reads=[self.pr[lb]],
                              writes=[rl_r])
                        fw.op(fw.dve, lambda: nc.vector.tensor_tensor(oh[:, qs0:qs0 + 512], self.pf[ob][:, :], rl[:],
                                                                      ALU.mult),
                              reads=[self.pr[ob], rl_r], writes=[oh_r])
                    fw.dma(fw.sp, oh_s, [(self.ot_scr[h], oh[:])], reads=[oh_r])
                fw.barrier()
        if self.ck("m2"):
            return
        self.wo_phase(self.mla_w_o, x_in, x_out)

    def dil_mixer(self, x_in, x_out):
        nc, fw = self.nc, self.fw
        cols = self.cols
        DIL = (1, 4, 16)
        scale = 1.0 / math.sqrt(128.0)
        with ExitStack() as st:
            cos128 = st.enter_context(nc.sbuf_tensor("cos128", [128, S], BF16))
            sin128 = st.enter_context(nc.sbuf_tensor("sin128", [128, S], BF16))
            tab_res = Res()
            self.rope_tables(128, cols[:, 18:19], cols[:, 19:20], cos128, sin128, tab_res)
            hT = st.enter_context(nc.sbuf_tensor("hT_d", [128, 16, S], BF16))
            hT_res = [[Res(), Res()] for _ in range(16)]
            w_ring = Ring(self, st, "wqkv", 3, [128, 16, 384], BF16, dma=True)
            wv = self.w_qkv.rearrange("(k p) e -> p k e", p=128)

            def load_w(hh):
                wts = []
                for c in range(3):
                    wt, wt_r, wt_s = w_ring.next()
                    e0 = (hh * 9 + c * 3) * 128
                    fw.dma(fw.pool, wt_s, [(wt[:, j * 8:(j + 1) * 8, :], wv[:, j * 8:(j + 1) * 8, e0:e0 + 384])
                                           for j in range(2)], writes=[wt_r])
                    wts.append((wt, wt_r))
                return wts
            pre_wts = load_w(0)
            self.norm_to_hT(st, x_in, self.mixer_norm[1:2, :], hT, hT_res, 0, 16)
            all_h = [r for tr in hT_res for r in tr]
            self.make_post(st)
            qk = [[st.enter_context(nc.sbuf_tensor("qk%d%d" % (c, g), [128, S], BF16)) for g in range(3)]
                  for c in range(2)]
            qk_res = [[Res() for g in range(3)] for c in range(2)]
            vv = [st.enter_context(nc.sbuf_tensor("vv%d" % g, [128, 16, 128], BF16)) for g in range(3)]
            vv_res = [Res() for g in range(3)]
            accO = st.enter_context(nc.sbuf_tensor("accO", [128, S], F32))
            accL = st.enter_context(nc.sbuf_tensor("accL", [128, S], F32))
            acc_res = Res()
            pt_ring = Ring(self, st, "ptd", LOOKAHEAD + 2, [128, 256], BF16)
            oh_ring = Ring(self, st, "ohd", 2, [128, S], BF16, dma=True)
            nst = 0
            pend_final = None
            for hh in range(16):
                wts = pre_wts if hh == 0 else load_w(hh)
                nb = 0
                for c in range(2):
                    wt, wt_r = wts[c]
                    for g in range(3):
                        for tb in range(4):
                            ts = slice(tb * 512, (tb + 1) * 512)
                            hr = [r for ti in range(tb * 4, tb * 4 + 4) for r in hT_res[ti]]
                            bank = nb % 2
                            nb += 1
                            pf = self.pf[bank]

                            def mm():
                                ins = None
                                for k in range(16):
                                    ins = nc.tensor.matmul(pf[:, :], wt[:, k, g * 128:(g + 1) * 128], hT[:, k, ts],
                                                           start=(k == 0), stop=(k == 15))
                                return ins
                            fw.op(fw.pe, mm, reads=[wt_r] + hr, writes=[self.pr[bank]])
                            self.qk_post(128, bank, cols[:, 12 + c * 3 + g:13 + c * 3 + g], qk[c][g][:, ts],
                                         qk_res[c][g],
                                         rope=(cos128[:, ts], sin128[:, ts], self.perm128, tab_res))
                            if nb == 3 and pend_final is not None:
                                pend_final()
                                pend_final = None
                self.flush_post()
                wt, wt_r = wts[2]
                for g in range(3):
                    dil = DIL[g]
                    nlb = 16 // dil
                    for cg in range(4):
                        bank = nb % 2
                        nb += 1
                        pf = self.pf[bank]

                        def mmv():
                            ins = None
                            for j in range(4):
                                ci = cg * 4 + j
                                r, lbk = ci // nlb, ci % nlb
                                t_lo = lbk * 128 * dil + r
                                tsl = slice(t_lo, t_lo + 127 * dil + 1, dil)
                                for k in range(16):
                                    ins = nc.tensor.matmul(pf[:, j * 128:(j + 1) * 128], hT[:, k, tsl],
                                                           wt[:, k, g * 128:(g + 1) * 128],
                                                           start=(j == 0 and k == 0), stop=(k == 15),
                                                           skip_group_check=True)
                            return ins
                        fw.op(fw.pe, mmv, reads=[wt_r] + all_h, writes=[self.pr[bank]])
                        fw.op(fw.act, lambda: nc.scalar.copy(vv[g][:, cg * 4:(cg + 1) * 4, :],
                                                             pf[:, :].rearrange("p (j d) -> p j d", j=4)),
                              reads=[self.pr[bank]], writes=[vv_res[g]])
                for g in range(3):
                    dil = DIL[g]
                    nlb = 16 // dil
                    qT, kT = qk[0][g], qk[1][g]
                    started = {}

                    def pv(ci_q, pt_ap, kchunk, pt_r):
                        sbk = ci_q // 4
                        ob = 4 + (sbk % 2)
                        lb_ = 6 + (sbk % 2)
                        cc = (ci_q % 4) * 128
                        first = not started.get(sbk, False)
                        started[sbk] = True

                        def f():
                            nc.tensor.matmul(self.pf[ob][:, cc:cc + 128], vv[g][:, kchunk, :], pt_ap,
                                             start=first, stop=False, skip_group_check=True)
                            return nc.tensor.matmul(self.pf[lb_][:, cc:cc + 128], self.ones, pt_ap,
                                                    start=first, stop=False, skip_group_check=True)
                        fw.op(fw.pe, f, reads=[vv_res[g], pt_r, self.c_res], writes=[self.pr[ob], self.pr[lb_]])

                    def flush(sbk):
                        ob = 4 + (sbk % 2)
                        lb_ = 6 + (sbk % 2)
                        if dil == 1:
                            oa = accO[:, sbk * 512:(sbk + 1) * 512]
                            la = accL[:, sbk * 512:(sbk + 1) * 512]
                            po = self.pf[ob][:, :]
                            pl = self.pf[lb_][:, :]
                        elif dil == 4:
                            oa = accO[:, sbk:S:4]
                            la = accL[:, sbk:S:4]
                            po = self.pf[ob][:, :]
                            pl = self.pf[lb_][:, :]
                        else:
                            oa = accO[:].rearrange("p (a r) -> p r a", r=16)[:, sbk * 4:(sbk + 1) * 4, :]
                            la = accL[:].rearrange("p (a r) -> p r a", r=16)[:, sbk * 4:(sbk + 1) * 4, :]
                            po = self.pf[ob][:, :].rearrange("p (r a) -> p r a", r=4)
                            pl = self.pf[lb_][:, :].rearrange("p (r a) -> p r a", r=4)
                        if g == 0:
                            fw.op(fw.dve, lambda: nc.vector.tensor_copy(oa, po), reads=[self.pr[ob]],
                                  writes=[acc_res])
                            fw.op(fw.act, lambda: nc.scalar.copy(la, pl), reads=[self.pr[lb_]], writes=[acc_res])
                        else:
                            fw.op(fw.dve, lambda: nc.vector.tensor_tensor(oa, po, oa, ALU.add),
                                  reads=[self.pr[ob], acc_res], writes=[acc_res])
                            fw.op(fw.dve, lambda: nc.vector.tensor_tensor(la, pl, la, ALU.add),
                                  reads=[self.pr[lb_], acc_res], writes=[acc_res])

                    pend_tail = []
                    for ci in range(16):
                        r, lbk = ci // nlb, ci % nlb
                        has_next = lbk + 1 < nlb
                        nq = 256 if has_next else 128
                        t_lo = lbk * 128 * dil + r
                        ksl = slice(t_lo, t_lo + 127 * dil + 1, dil)
                        qsl = slice(t_lo, t_lo + (nq - 1) * dil + 1, dil)
                        sb = ST_BANKS[nst % len(ST_BANKS)]
                        nst += 1
                        fw.op(fw.pe, lambda: nc.tensor.matmul(self.pf[sb][:, 0:nq], kT[:, ksl], qT[:, qsl],
                                                              start=True, stop=True),
                              reads=[qk_res[0][g], qk_res[1][g]], writes=[self.pr[sb]])
                        pt, pt_r, _ = pt_ring.next()
                        fw.op(fw.act, lambda: nc.scalar.activation(pt[:, 0:nq], self.pf[sb][:, 0:nq], AF.Exp,
                                                                   bias=self.zero_col, scale=scale),
                              reads=[self.pr[sb], self.c_res], writes=[pt_r])
                        fw.op(fw.dve, lambda: nc.vector.tensor_tensor(pt[:, 0:nq], pt[:, 0:nq], self.mask2[:, 0:nq],
                                                                      ALU.mult),
                              reads=[pt_r, self.c_res], writes=[pt_r])
                        def mk_tail(ci=ci, pt=pt, pt_r=pt_r, has_next=has_next):
                            def tail():
                                pv(ci, pt[:, 0:128], ci, pt_r)
                                if has_next:
                                    pv(ci + 1, pt[:, 128:256], ci, pt_r)
                                if ci % 4 == 3:
                                    flush(ci // 4)
                            return tail
                        pend_tail.append(mk_tail())
                        while len(pend_tail) > LOOKAHEAD:
                            pend_tail.pop(0)()
                    while pend_tail:
                        pend_tail.pop(0)()
                def mk_final(hh=hh):
                    def fin():
                        oh, oh_r, oh_s = oh_ring.next()
                        fw.op(fw.dve, lambda: nc.vector.reciprocal(accL[:], accL[:]), reads=[acc_res],
                              writes=[acc_res])
                        fw.op(fw.dve, lambda: nc.vector.tensor_tensor(oh[:], accO[:], accL[:], ALU.mult),
                              reads=[acc_res], writes=[oh_r])
                        fw.dma(fw.sp, oh_s, [(self.ot_scr[hh], oh[:])], reads=[oh_r])
                    return fin
                pend_final = mk_final()
            if pend_final is not None:
                pend_final()
                pend_final = None
            fw.barrier()
        self.wo_phase(self.dil_w_o, x_in, x_out)


def _consts():
    bf = ml_dtypes.bfloat16
    cb = np.zeros((128, 768), np.float32)
    cb[:, 0:128] = np.eye(128)
    cb[:, 128:256] = 1.0
    dp = np.arange(128)
    perm = np.zeros((128, 128), np.float32)
    perm[(dp + 64) % 128, dp] = 1.0
    cb[:, 256:384] = perm
    dp = np.arange(64)
    perm64 = np.zeros((64, 64), np.float32)
    perm64[(dp + 32) % 64, dp] = 1.0
    cb[0:64, 384:448] = perm64
    a = np.arange(128)[:, None]
    b = np.arange(128)[None, :]
    cb[:, 512:640] = (a <= b)
    cb[:, 640:768] = (a >= b)
    return cb.astype(bf)


def _cols(inp):
    c = np.zeros((128, 32), np.float32)
    c[:, 0:4] = inp["mla_q_norm"][0].reshape(4, 128).T
    c[:, 4:8] = inp["mla_kv_norm"][0].reshape(4, 128).T
    c[:, 8] = inp["mla_q_gain"][0][:128]
    c[:64, 9] = inp["mla_q_gain"][0][128:]
    c[:, 10] = inp["mla_k_gain"][0][:128]
    c[:64, 11] = inp["mla_k_gain"][0][128:]
    c[:, 12:15] = inp["dil_q_gain"][0].T
    c[:, 15:18] = inp["dil_k_gain"][0].T
    p = np.arange(128)
    c[:, 18] = np.power(np.float32(10000.0), (-2.0 * (p % 64).astype(np.float32) / np.float32(128.0))).astype(np.float32)
    c[:, 19] = np.where(p < 64, -1.0, 1.0)
    p = np.arange(64)
    c[:64, 20] = np.power(np.float32(10000.0), (-2.0 * (p % 32).astype(np.float32) / np.float32(64.0))).astype(np.float32)
    c[:64, 21] = np.where(p < 32, -1.0, 1.0)
    return c


def make_in_maps(inp, cores):
    cb = _consts()
    cols = _cols(inp)
    f = lambda a: np.ascontiguousarray(np.asarray(a, dtype=np.float32))
    wqkv_r = np.ascontiguousarray(
        np.asarray(inp["dil_w_qkv"][0], np.float32).reshape(D, 9, 16, 128).transpose(0, 2, 1, 3).reshape(D, 18432))
    shared = {
        "mixer_norm": f(inp["mixer_norm"]), "ffn_norm": f(inp["ffn_norm"]),
        "mla_w_down": f(inp["mla_w_down"][0]), "mla_w_uq": f(inp["mla_w_uq"][0]),
        "mla_w_ukv": f(inp["mla_w_ukv"][0]), "mla_w_o": f(inp["mla_w_o"][0]),
        "dil_w_qkv_r": wqkv_r, "dil_w_o": f(inp["dil_w_o"][0]),
        "ffn_w_gate": f(inp["ffn_w_gate"]), "ffn_w_up": f(inp["ffn_w_up"]), "ffn_w_down": f(inp["ffn_w_down"]),
        "cols": cols, "cb": cb,
    }
    maps = []
    for b in cores:
        m = dict(shared)
        m["x"] = f(inp["x"][b])
        m["pos"] = np.ascontiguousarray(np.asarray(inp["positions"][b], np.int32).reshape(1, S))
        maps.append(m)
    return maps


def kernel(**inputs):
    prog = Prog()
    maps = make_in_maps(inputs, list(range(8)))
    res = run_bass_kernel_spmd(prog.nc, maps, core_ids=list(range(8)))
    return np.stack([np.asarray(r["out"], np.float32) for r in res.results], axis=0)
```

```python
import math
from contextlib import ExitStack
import numpy as np
import ml_dtypes
import concourse.bass as bass
import concourse.mybir as mybir
from concourse.bass_utils import run_bass_kernel_spmd

F32 = mybir.dt.float32
BF16 = mybir.dt.bfloat16
I32 = mybir.dt.int32
AF = mybir.ActivationFunctionType
ALU = mybir.AluOpType

SAME_ENGINE_SYNC = True
LOOKAHEAD = 3
POST_DEFER = 2
ST_BANKS = (2, 3, 0, 1)
NOSYNC = ()
D = 2048
S = 2048
FF = 5632
EPS = 1e-6
PI = math.pi
TWO_PI = 2.0 * math.pi
C1 = 6.28125
C2 = TWO_PI - C1


class Sem:
    def __init__(self, nc, name):
        self.h = nc.alloc_semaphore(name)
        self.count = 0


class Res:
    __slots__ = ("last_w", "readers")

    def __init__(self):
        self.last_w = None
        self.readers = []


class Eng:
    def __init__(self, nc, name, eng):
        self.name = name
        self.eng = eng
        self.sem = Sem(nc, "s_" + name)
        self.seen = {}
        self.is_pe = name == "pe"


class FW:
    def __init__(self, nc):
        self.nc = nc
        self.pe = Eng(nc, "pe", nc.tensor)
        self.act = Eng(nc, "act", nc.scalar)
        self.dve = Eng(nc, "dve", nc.vector)
        self.pool = Eng(nc, "pool", nc.gpsimd)
        self.sp = Eng(nc, "sp", nc.sync)
        self.engs = [self.pe, self.act, self.dve, self.pool, self.sp]
        self.dma_sems = {}
        self.n_inst = 0
        self._uid = 0

    def uid(self):
        self._uid += 1
        return self._uid

    def dsem(self, name):
        if name not in self.dma_sems:
            self.dma_sems[name] = Sem(self.nc, "d_" + name)
        return self.dma_sems[name]

    def _waits(self, E, reads, writes):
        deps = {}
        for r in reads:
            if r.last_w is not None:
                s, v = r.last_w
                if deps.get(s, 0) < v:
                    deps[s] = v
        for w in writes:
            if w.last_w is not None:
                s, v = w.last_w
                if deps.get(s, 0) < v:
                    deps[s] = v
            for (s, v) in w.readers:
                if deps.get(s, 0) < v:
                    deps[s] = v
        for s, v in deps.items():
            if s is E.sem and (E.is_pe or not SAME_ENGINE_SYNC or E.name in NOSYNC):
                continue
            if E.seen.get(s, 0) >= v:
                continue
            E.eng.wait_ge(s.h, v)
            self.n_inst += 1
            E.seen[s] = v

    def _commit(self, tag, reads, writes):
        for r in reads:
            r.readers.append(tag)
        for w in writes:
            w.last_w = tag
            w.readers = []

    def op(self, E, fn, reads=(), writes=()):
        self._waits(E, reads, writes)
        inst = fn()
        E.sem.count += 1
        inst.then_inc(E.sem.h, 1)
        self.n_inst += 1
        tag = (E.sem, E.sem.count)
        self._commit(tag, reads, writes)
        return tag

    def dma(self, E, sem, pairs, reads=(), writes=()):
        self._waits(E, reads, writes)
        for (o, i) in pairs:
            E.eng.dma_start(out=o, in_=i).then_inc(sem.h, 16)
            sem.count += 16
            self.n_inst += 1
        tag = (sem, sem.count)
        self._commit(tag, reads, writes)
        return tag

    def barrier(self):
        sems = [e.sem for e in self.engs] + list(self.dma_sems.values())
        for E in self.engs:
            for s in sems:
                if s is E.sem:
                    continue
                if s.count > 0 and E.seen.get(s, 0) < s.count:
                    E.eng.wait_ge(s.h, s.count)
                    E.seen[s] = s.count
                    self.n_inst += 1


class Ring:
    def __init__(self, K, stack, name, n, shape, dtype, dma=False):
        self.t = [stack.enter_context(K.nc.sbuf_tensor("%s%d_%d" % (name, i, K.fw.uid()), shape, dtype))
                  for i in range(n)]
        self.r = [Res() for _ in range(n)]
        self.s = [K.fw.dsem("%s%d" % (name, i)) for i in range(n)] if dma else [None] * n
        self.i = -1
        self.n = n

    def next(self):
        self.i = (self.i + 1) % self.n
        return self.t[self.i], self.r[self.i], self.s[self.i]


class Stop(Exception):
    pass


class Prog:
    def ck(self, name):
        if self.stop == name:
            self.fw.barrier()
            self.stopped = True
            return True
        return False

    def __init__(self, debug=False, stop=None, lite=False, skip=()):
        self.debug = debug
        self.stop = stop
        self.stopped = False
        nc = self.nc = bass.Bass("TRN2", target_bir_lowering=False)
        self.fw = FW(nc)
        fw = self.fw

        self.skip = skip

        def din(name, shape, dt=F32):
            if lite and name not in ("x", "pos", "mixer_norm", "ffn_norm", "cols", "cb") + tuple(lite):
                shape = [2, 2]
            return nc.dram_tensor(name, shape, dt, kind="ExternalInput").ap()

        self.x = din("x", [S, D])
        self.pos = din("pos", [1, S], I32)
        self.mixer_norm = din("mixer_norm", [2, D])
        self.ffn_norm = din("ffn_norm", [2, D])
        self.w_down = din("mla_w_down", [D, 1088])
        self.w_uq = din("mla_w_uq", [512, 3072])
        self.w_ukv = din("mla_w_ukv", [512, 4096])
        self.mla_w_o = din("mla_w_o", [D, D])
        self.w_qkv = din("dil_w_qkv_r", [D, 16 * 9 * 128])
        self.dil_w_o = din("dil_w_o", [D, D])
        self.w_gate = din("ffn_w_gate", [2, D, FF])
        self.w_up = din("ffn_w_up", [2, D, FF])
        self.w_dn = din("ffn_w_down", [2, FF, D])
        self.cols_d = din("cols", [128, 32])
        self.cb_d = din("cb", [128, 768], BF16)
        kind = "ExternalOutput" if debug else "Internal"
        self.x1 = nc.dram_tensor("x1", [S, D], F32, kind=kind).ap()
        self.x2 = nc.dram_tensor("x2", [S, D], F32, kind=kind).ap()
        self.x3 = nc.dram_tensor("x3", [S, D], F32, kind=kind).ap()
        self.out = nc.dram_tensor("out", [S, D], F32, kind="ExternalOutput").ap()
        self.ot_scr = nc.dram_tensor("ot_scr", [16, 128, S], BF16, kind=kind).ap()

        self.pb = [nc.alloc_psum_tensor("pb%d" % i, [128, 1024], BF16) for i in range(8)]
        self.pf = [b.bitcast(F32) for b in self.pb]
        self.pr = [Res() for _ in range(8)]

        with ExitStack() as st:
            self.cols = st.enter_context(nc.sbuf_tensor("cols_sb", [128, 32], F32))
            self.cb = st.enter_context(nc.sbuf_tensor("cb_sb", [128, 768], BF16))
            self.cst = st.enter_context(nc.sbuf_tensor("cst", [128, 4], F32))
            self.c_res = Res()
            fw.dma(fw.sp, fw.dsem("c0"), [(self.cols[:], self.cols_d)], writes=[self.c_res])
            fw.dma(fw.sp, fw.dsem("c1"), [(self.cb[:], self.cb_d)], writes=[self.c_res])
            fw.op(fw.dve, lambda: nc.vector.memset(self.cst[:, 0:1], EPS), writes=[self.c_res])
            fw.op(fw.dve, lambda: nc.vector.memset(self.cst[:, 1:2], 0.0), writes=[self.c_res])
            self.ident = self.cb[:, 0:128]
            self.ones = self.cb[:, 128:256]
            self.perm128 = self.cb[:, 256:384]
            self.perm64 = self.cb[0:64, 384:448]
            self.mask_cur = self.cb[:, 512:640]
            self.mask2 = self.cb[:, 512:768]
            self.eps_col = self.cst[:, 0:1]
            self.zero_col = self.cst[:, 1:2]
            fw.barrier()
            try:
                self.body()
            except Stop:
                pass
            fw.barrier()

    def body(self):
        stop = self.stop
        self.mla_mixer(self.x, self.x1)
        if stop == "mla" or self.stopped:
            return
        self.ffn(0, self.x1, self.x2)
        if stop == "ffn0":
            return
        self.dil_mixer(self.x2, self.x3)
        if stop == "dil" or self.stopped:
            return
        self.ffn(1, self.x3, self.out)

    def norm_to_hT(self, st_outer, x_dram, gain_row, hT, hT_res, t0, ntile):
        nc, fw = self.nc, self.fw
        with ExitStack() as st:
            gain = st.enter_context(nc.sbuf_tensor("gain_%d" % fw.uid(), [128, D], F32))
            g_res = Res()
            fw.dma(fw.sp, fw.dsem("gain"), [(gain[:], gain_row.partition_broadcast(128))], writes=[g_res])
            xt_ring = Ring(self, st, "xt", 2, [128, D], F32, dma=True)
            xn_ring = Ring(self, st, "xn", 2, [128, D], BF16)
            junk_ring = Ring(self, st, "junk", 1, [128, D], BF16)
            ss_ring = Ring(self, st, "nss", 2, [128, 4], F32)
            for ti in range(ntile):
                tok = t0 + ti * 128
                xt, xt_r, xt_s = xt_ring.next()
                fw.dma(fw.sp, xt_s, [(xt[:], x_dram[tok:tok + 128, :])], writes=[xt_r])
                jk, jk_r, _ = junk_ring.next()
                ss, ss_r, _ = ss_ring.next()
                fw.op(fw.act, lambda: nc.scalar.activation(jk[:], xt[:], AF.Square, accum_out=ss[:, 0:1]),
                      reads=[xt_r], writes=[jk_r, ss_r])
                fw.op(fw.act, lambda: nc.scalar.activation(ss[:, 1:2], ss[:, 0:1], AF.Ln, bias=self.eps_col,
                                                           scale=1.0 / D),
                      reads=[ss_r, self.c_res], writes=[ss_r])
                fw.op(fw.act, lambda: nc.scalar.activation(ss[:, 2:3], ss[:, 1:2], AF.Exp, bias=self.zero_col,
                                                           scale=-0.5),
                      reads=[ss_r], writes=[ss_r])
                xn, xn_r, _ = xn_ring.next()
                fw.op(fw.dve, lambda: nc.vector.scalar_tensor_tensor(xn[:], xt[:], ss[:, 2:3], gain[:],
                                                                     ALU.mult, ALU.mult),
                      reads=[xt_r, ss_r, g_res], writes=[xn_r])
                for half in range(2):
                    bank = half
                    pbk = self.pb[bank]

                    def tr():
                        ins = None
                        for j in range(8):
                            k = half * 8 + j
                            ins = nc.tensor.transpose(pbk[:, j * 128:(j + 1) * 128], xn[:, k * 128:(k + 1) * 128],
                                                      self.ident)
                        return ins
                    fw.op(fw.pe, tr, reads=[xn_r, self.c_res], writes=[self.pr[bank]])
                    src = pbk[:].rearrange("p (k t) -> p k t", k=8)
                    dst = hT[:, half * 8:(half + 1) * 8, ti * 128:(ti + 1) * 128]
                    if half == 0:
                        fw.op(fw.act, lambda: nc.scalar.copy(dst, src), reads=[self.pr[bank]],
                              writes=[hT_res[ti][half]])
                    else:
                        fw.op(fw.dve, lambda: nc.vector.tensor_copy(dst, src), reads=[self.pr[bank]],
                              writes=[hT_res[ti][half]])
            fw.barrier()

    def proj_residual(self, aT, aT_reads, nk, ntile, w_dram, res_dram, out_dram, t0, ncol_blk, wname):
        nc, fw = self.nc, self.fw
        cw = D // ncol_blk
        with ExitStack() as st:
            nsplit = 4 if nk >= 16 else 1
            kk = nk // nsplit
            w_rings = [Ring(self, st, "%sq%d" % (wname, j), 2, [128, kk, cw], BF16, dma=True) for j in range(nsplit)]
            xr_ring = Ring(self, st, "xr", 3, [128, cw], F32, dma=True)
            o_ring = Ring(self, st, "ost", 3, [128, cw], F32, dma=True)
            wv = w_dram.rearrange("(k p) d -> p k d", p=128)
            nb = 0
            for cb_i in range(ncol_blk):
                c0 = cb_i * cw
                parts = []
                for j in range(nsplit):
                    wt, wt_r, wt_s = w_rings[j].next()
                    fw.dma(fw.pool, wt_s, [(wt[:], wv[:, j * kk:(j + 1) * kk, c0:c0 + cw])], writes=[wt_r])
                    parts.append((wt, wt_r))
                for ti in range(ntile):
                    tok = t0 + ti * 128
                    bank = 2 + (nb % 2)
                    nb += 1
                    pf = self.pf[bank]
                    for j in range(nsplit):
                        wt, wt_r = parts[j]

                        def mm():
                            ins = None
                            for kq in range(kk):
                                k = j * kk + kq
                                ins = nc.tensor.matmul(pf[:, 0:cw], aT[:, k, ti * 128:(ti + 1) * 128], wt[:, kq, :],
                                                       start=(k == 0), stop=(k == nk - 1))
                            return ins
                        fw.op(fw.pe, mm, reads=[wt_r] + aT_reads(ti), writes=[self.pr[bank]])
                    xr, xr_r, xr_s = xr_ring.next()
                    fw.dma(fw.sp, xr_s, [(xr[:], res_dram[tok:tok + 128, c0:c0 + cw])], writes=[xr_r])
                    ot, ot_r, ot_s = o_ring.next()
                    fw.op(fw.dve, lambda: nc.vector.tensor_tensor(ot[:], pf[:, 0:cw], xr[:], ALU.add),
                          reads=[self.pr[bank], xr_r], writes=[ot_r])
                    fw.dma(fw.sp, ot_s, [(out_dram[tok:tok + 128, c0:c0 + cw], ot[:])], reads=[ot_r])
            fw.barrier()

    def ffn(self, layer, x_in, x_out):
        nc, fw = self.nc, self.fw
        NT = 1024
        for half in range(2):
            t0 = half * NT
            with ExitStack() as st:
                aT = st.enter_context(nc.sbuf_tensor("aT_%d" % fw.uid(), [128, 44, NT], BF16))
                aT_res = [[Res() for _ in range(2)] for _ in range(44)]
                with ExitStack() as st2:
                    hT = st2.enter_context(nc.sbuf_tensor("hTf_%d" % fw.uid(), [128, 16, NT], BF16))
                    hT_res = [[Res(), Res()] for _ in range(NT // 128)]
                    self.norm_to_hT(st2, x_in, self.ffn_norm[layer:layer + 1, :], hT, hT_res, t0, NT // 128)
                    w_ring = Ring(self, st2, "wgu", 3, [128, 2, 16, 256], BF16, dma=True)
                    sg_ring = Ring(self, st2, "sg", 2, [128, 512], F32)
                    wg_v = self.w_gate[layer].rearrange("(k p) f -> p k f", p=128)
                    wu_v = self.w_up[layer].rearrange("(k p) f -> p k f", p=128)
                    nb = 0
                    for fc2 in range(22):
                        wt, wt_r, wt_s = w_ring.next()
                        f0 = fc2 * 256
                        fw.dma(fw.pool, wt_s, [(wt[:, 0], wg_v[:, :, f0:f0 + 256]),
                                               (wt[:, 1], wu_v[:, :, f0:f0 + 256])], writes=[wt_r])
                        for sub in range(2):
                            fc = fc2 * 2 + sub
                            for tb in range(NT // 512):
                                hr = [r for ti in range(tb * 4, tb * 4 + 4) for r in hT_res[ti]]
                                banks = [2 + (nb % 2) * 2, 3 + (nb % 2) * 2]
                                nb += 1
                                for gu in range(2):
                                    pf = self.pf[banks[gu]]

                                    def mm():
                                        ins = None
                                        for k in range(16):
                                            ins = nc.tensor.matmul(pf[:, :], wt[:, gu, k, sub * 128:(sub + 1) * 128],
                                                                   hT[:, k, tb * 512:(tb + 1) * 512],
                                                                   start=(k == 0), stop=(k == 15))
                                        return ins
                                    fw.op(fw.pe, mm, reads=[wt_r] + hr, writes=[self.pr[banks[gu]]])
                                sg, sg_r, _ = sg_ring.next()
                                fw.op(fw.act, lambda: nc.scalar.activation(sg[:], self.pf[banks[0]][:, :], AF.Silu),
                                      reads=[self.pr[banks[0]]], writes=[sg_r])
                                fw.op(fw.dve, lambda: nc.vector.tensor_tensor(aT[:, fc, tb * 512:(tb + 1) * 512],
                                                                              self.pf[banks[1]][:, :], sg[:],
                                                                              ALU.mult),
                                      reads=[self.pr[banks[1]], sg_r], writes=[aT_res[fc][tb]])
                    fw.barrier()
                allr = [r for fr in aT_res for r in fr]
                self.proj_residual(aT, lambda ti: allr, 44, NT // 128, self.w_dn[layer], x_in, x_out, t0, 4, "wdn")

    def rope_tables(self, Dh, invf_col, sgn_col, cosT, sinS, tab_res):
        nc, fw = self.nc, self.fw
        with ExitStack() as st:
            posi = st.enter_context(nc.sbuf_tensor("posi_%d" % fw.uid(), [128, S], I32))
            posf = st.enter_context(nc.sbuf_tensor("posf_%d" % fw.uid(), [128, S], F32))
            ang = st.enter_context(nc.sbuf_tensor("ang_%d" % fw.uid(), [128, 512], F32))
            v = st.enter_context(nc.sbuf_tensor("v_%d" % fw.uid(), [128, 512], F32))
            ki = st.enter_context(nc.sbuf_tensor("ki_%d" % fw.uid(), [128, 512], I32))
            kf = st.enter_context(nc.sbuf_tensor("kf_%d" % fw.uid(), [128, 512], F32))
            m = st.enter_context(nc.sbuf_tensor("m_%d" % fw.uid(), [128, 512], F32))
            R = Res()
            fw.dma(fw.sp, fw.dsem("posi"), [(posi[:], self.pos.partition_broadcast(128))], writes=[R])
            V = nc.vector

            def dv(fn):
                fw.op(fw.dve, fn, reads=[R, self.c_res], writes=[R])
            if "conv0" in self.skip:
                dv(lambda: V.memset(posf[:], 3.0))
            else:
                dv(lambda: V.tensor_copy(posf[:], posi[:]))
            P = slice(0, Dh)
            for tb in range(4 if "short" not in self.skip else 1):
                cs = slice(tb * 512, (tb + 1) * 512)
                for which in range(2):
                    dv(lambda: V.tensor_scalar(ang[P], posf[P, cs], invf_col, None, ALU.mult))
                    if which == 1:
                        dv(lambda: V.tensor_scalar(ang[P], ang[P], PI / 2, None, ALU.add))
                    dv(lambda: V.tensor_scalar(v[P], ang[P], 1.0 / TWO_PI, None, ALU.mult))
                    if "conv" in self.skip:
                        dv(lambda: V.tensor_copy(kf[P], v[P]))
                    else:
                        dv(lambda: V.tensor_copy(ki[P], v[P]))
                        dv(lambda: V.tensor_copy(kf[P], ki[P]))
                    dv(lambda: V.scalar_tensor_tensor(ang[P], kf[P], -C1, ang[P], ALU.mult, ALU.add))
                    dv(lambda: V.scalar_tensor_tensor(ang[P], kf[P], -C2, ang[P], ALU.mult, ALU.add))
                    if "cmp" not in self.skip:
                        dv(lambda: V.tensor_scalar(m[P], ang[P], PI, -TWO_PI, ALU.is_gt, ALU.mult))
                        dv(lambda: V.tensor_tensor(ang[P], ang[P], m[P], ALU.add))
                        dv(lambda: V.tensor_scalar(m[P], ang[P], -PI, TWO_PI, ALU.is_lt, ALU.mult))
                        dv(lambda: V.tensor_tensor(ang[P], ang[P], m[P], ALU.add))
                    dv(lambda: V.tensor_scalar(ang[P], ang[P], 3.1415925, -3.1415925, ALU.min, ALU.max))
                    fw.op(fw.act, lambda: nc.scalar.activation(v[P], ang[P], AF.Sin if "sin" not in self.skip else AF.Copy,
                                                               bias=self.zero_col[P], scale=1.0),
                          reads=[R, self.c_res], writes=[R])
                    if which == 0:
                        dv(lambda: V.tensor_scalar(sinS[P, cs], v[P], sgn_col, None, ALU.mult))
                    else:
                        dv(lambda: V.tensor_copy(cosT[P, cs], v[P]))
            fw.op(fw.dve, lambda: V.tensor_copy(m[P, 0:1], m[P, 0:1]), reads=[R], writes=[R, tab_res])
            fw.barrier()

    def make_post(self, st):
        self.sq_ring = Ring(self, st, "sq", POST_DEFER + 1, [128, 512], BF16)
        self.qg_ring = Ring(self, st, "qg", POST_DEFER + 1, [128, 512], BF16)
        self.ln_ring = Ring(self, st, "lnv", 2, [128, 512], F32)
        self.rs_ring = Ring(self, st, "rstd", 2, [128, 512], F32)
        self.t1_ring = Ring(self, st, "t1", 2, [128, 512], F32)
        self.t2_ring = Ring(self, st, "t2", 2, [128, 512], F32)

    def rstd_from_ss(self, Dh, ss_bank, nfeat):
        nc, fw = self.nc, self.fw
        P = slice(0, Dh)
        ln, ln_r, _ = self.ln_ring.next()
        fw.op(fw.act, lambda: nc.scalar.activation(ln[P], self.pf[ss_bank][P, :], AF.Ln, bias=self.eps_col[P],
                                                   scale=1.0 / nfeat),
              reads=[self.pr[ss_bank], self.c_res], writes=[ln_r])
        rs, rs_r, _ = self.rs_ring.next()
        fw.op(fw.act, lambda: nc.scalar.activation(rs[P], ln[P], AF.Exp, bias=self.zero_col[P], scale=-0.5),
              reads=[ln_r, self.c_res], writes=[rs_r])
        return rs, rs_r

    def qk_post(self, Dh, raw_bank, gain_col, out_ap, out_res, rope=None, ss_bank=6, rot_bank=7):
        nc, fw = self.nc, self.fw
        P = slice(0, Dh)
        raw = self.pf[raw_bank]
        rr = self.pr[raw_bank]
        pend = self.__dict__.setdefault("pending_posts", [])
        while len(pend) >= POST_DEFER:
            pend.pop(0)()
        sq, sq_r, _ = self.sq_ring.next()
        fw.op(fw.act, lambda: nc.scalar.activation(sq[P], raw[P, :], AF.Square), reads=[rr], writes=[sq_r])
        qg, qg_r, _ = self.qg_ring.next()
        fw.op(fw.dve, lambda: nc.vector.tensor_scalar(qg[P], raw[P, :], gain_col, None, ALU.mult),
              reads=[rr, self.c_res], writes=[qg_r, rr])
        pend.append(lambda: self._post_b(Dh, sq, sq_r, qg, qg_r, out_ap, out_res, rope, ss_bank, rot_bank))

    def flush_post(self):
        pend = self.__dict__.setdefault("pending_posts", [])
        while pend:
            pend.pop(0)()

    def _post_b(self, Dh, sq, sq_r, qg, qg_r, out_ap, out_res, rope, ss_bank, rot_bank):
        nc, fw = self.nc, self.fw
        P = slice(0, Dh)
        fw.op(fw.pe, lambda: nc.tensor.matmul(self.pf[ss_bank][P, :], self.ones[P, 0:Dh], sq[P],
                                              start=True, stop=True),
              reads=[sq_r, self.c_res], writes=[self.pr[ss_bank]])
        if rope is not None:
            cos_ap, sin_ap, perm_ap, tab_res = rope
            fw.op(fw.pe, lambda: nc.tensor.matmul(self.pf[rot_bank][P, :], perm_ap, qg[P], start=True, stop=True),
                  reads=[qg_r, self.c_res], writes=[self.pr[rot_bank]])
        rs, rs_r = self.rstd_from_ss(Dh, ss_bank, Dh)
        if rope is None:
            fw.op(fw.dve, lambda: nc.vector.tensor_tensor(out_ap, qg[P], rs[P], ALU.mult),
                  reads=[qg_r, rs_r], writes=[out_res])
        else:
            t1, t1_r, _ = self.t1_ring.next()
            t2, t2_r, _ = self.t2_ring.next()
            fw.op(fw.dve, lambda: nc.vector.tensor_tensor(t1[P], qg[P], cos_ap, ALU.mult),
                  reads=[qg_r, tab_res], writes=[t1_r])
            fw.op(fw.dve, lambda: nc.vector.tensor_tensor(t2[P], self.pf[rot_bank][P, :], sin_ap, ALU.mult),
                  reads=[self.pr[rot_bank], tab_res], writes=[t2_r])
            fw.op(fw.dve, lambda: nc.vector.tensor_tensor(t1[P], t1[P], t2[P], ALU.add),
                  reads=[t1_r, t2_r], writes=[t1_r])
            fw.op(fw.dve, lambda: nc.vector.tensor_tensor(out_ap, t1[P], rs[P], ALU.mult),
                  reads=[t1_r, rs_r], writes=[out_res])

    def wo_phase(self, w_o, x_in, x_out):
        nc, fw = self.nc, self.fw
        with ExitStack() as st:
            OT = st.enter_context(nc.sbuf_tensor("OT_%d" % fw.uid(), [128, 16, S], BF16))
            R = Res()
            fw.dma(fw.sp, fw.dsem("otl"), [(OT[:, h, :], self.ot_scr[h]) for h in range(16)], writes=[R])
            self.proj_residual(OT, lambda ti: [R], 16, 16, w_o, x_in, x_out, 0, 4, "wo")

    def mla_mixer(self, x_in, x_out):
        nc, fw = self.nc, self.fw
        cols = self.cols
        with ExitStack() as st:
            cqT = st.enter_context(nc.sbuf_tensor("cqT", [128, 4, S], BF16))
            ckvT = st.enter_context(nc.sbuf_tensor("ckvT", [128, 4, S], BF16))
            krT = st.enter_context(nc.sbuf_tensor("krT", [64, S], BF16))
            cos64 = st.enter_context(nc.sbuf_tensor("cos64", [64, S], BF16))
            sin64 = st.enter_context(nc.sbuf_tensor("sin64", [64, S], BF16))
            tab_res = Res()
            cq_res = [Res() for _ in range(4)]
            ckv_res = [Res() for _ in range(4)]
            kr_res = [Res() for _ in range(4)]
            self.rope_tables(64, cols[0:64, 20:21], cols[0:64, 21:22], cos64, sin64, tab_res)
            if self.ck("tables"):
                return
            with ExitStack() as st1:
                hT = st1.enter_context(nc.sbuf_tensor("hT_m", [128, 16, S], BF16))
                hT_res = [[Res(), Res()] for _ in range(16)]
                wdn = st1.enter_context(nc.sbuf_tensor("wdn_m", [128, 16, 1088], BF16))
                wdn_r = Res()
                wv = self.w_down.rearrange("(k p) e -> p k e", p=128)
                fw.dma(fw.pool, fw.dsem("wdnm"), [(wdn[:, j * 4:(j + 1) * 4, :], wv[:, j * 4:(j + 1) * 4, :])
                                                  for j in range(4)], writes=[wdn_r])
                self.norm_to_hT(st1, x_in, self.mixer_norm[0:1, :], hT, hT_res, 0, 16)
                if self.ck("norm"):
                    return
                self.make_post(st1)
                cg_ring = Ring(self, st1, "cg", 2, [128, 4, 512], BF16)
                nb = 0
                for tb in range(4):
                    hr = [r for ti in range(tb * 4, tb * 4 + 4) for r in hT_res[ti]]
                    ts = slice(tb * 512, (tb + 1) * 512)
                    for grp in range(2):
                        cg, cg_r, _ = cg_ring.next()
                        ss_bank = 4 + grp
                        for c in range(4):
                            oc = grp * 4 + c
                            bank = nb % 3
                            nb += 1
                            pf = self.pf[bank]

                            def mm():
                                ins = None
                                for k in range(16):
                                    ins = nc.tensor.matmul(pf[:, :], wdn[:, k, oc * 128:(oc + 1) * 128], hT[:, k, ts],
                                                           start=(k == 0), stop=(k == 15))
                                return ins
                            fw.op(fw.pe, mm, reads=[wdn_r] + hr, writes=[self.pr[bank]])
                            sq, sq_r, _ = self.sq_ring.next()
                            fw.op(fw.act, lambda: nc.scalar.activation(sq[:], pf[:, :], AF.Square),
                                  reads=[self.pr[bank]], writes=[sq_r])
                            fw.op(fw.dve, lambda: nc.vector.tensor_scalar(cg[:, c, :], pf[:, :], cols[:, oc:oc + 1],
                                                                          None, ALU.mult),
                                  reads=[self.pr[bank], self.c_res], writes=[cg_r, self.pr[bank]])
                            fw.op(fw.pe, lambda: nc.tensor.matmul(self.pf[ss_bank][:, :], self.ones, sq[:],
                                                                  start=(c == 0), stop=(c == 3)),
                                  reads=[sq_r, self.c_res], writes=[self.pr[ss_bank]])
                        rs, rs_r = self.rstd_from_ss(128, ss_bank, 512)
                        dstT = cqT if grp == 0 else ckvT
                        dres = cq_res if grp == 0 else ckv_res
                        for c in range(4):
                            fw.op(fw.dve, lambda: nc.vector.tensor_tensor(dstT[:, c, ts], cg[:, c, :], rs[:],
                                                                          ALU.mult),
                                  reads=[cg_r, rs_r], writes=[dres[tb]])
                    bank = nb % 3
                    nb += 1
                    pf = self.pf[bank]

                    def mmr():
                        ins = None
                        for k in range(16):
                            ins = nc.tensor.matmul(pf[0:64, :], wdn[:, k, 1024:1088], hT[:, k, ts],
                                                   start=(k == 0), stop=(k == 15))
                        return ins
                    fw.op(fw.pe, mmr, reads=[wdn_r] + hr, writes=[self.pr[bank]])
                    self.qk_post(64, bank, cols[0:64, 11:12], krT[:, ts], kr_res[tb],
                                 rope=(cos64[:, ts], sin64[:, ts], self.perm64, tab_res))
                    self.flush_post()
                self.flush_post()
                fw.barrier()
            if self.ck("m1"):
                return
            scale = 1.0 / math.sqrt(192.0)
            with ExitStack() as st2:
                self.make_post(st2)
                wq_ring = Ring(self, st2, "wq", 2, [128, 4, 192], BF16, dma=True)
                wkv_ring = Ring(self, st2, "wkv", 2, [128, 4, 256], BF16, dma=True)
                qn_ring = Ring(self, st2, "qn", 2, [128, S], BF16)
                qr_ring = Ring(self, st2, "qr", 2, [64, S], BF16)
                kn_ring = Ring(self, st2, "kn", 2, [128, S], BF16)
                v_ring = Ring(self, st2, "vh", 2, [128, 16, 128], BF16)
                pt_ring = Ring(self, st2, "pt", LOOKAHEAD + 2, [128, 512], BF16)
                rl_ring = Ring(self, st2, "rl", 2, [128, 512], F32)
                oh_ring = Ring(self, st2, "oh", 2, [128, S], BF16, dma=True)
                wq_v = self.w_uq.rearrange("(k p) e -> p k e", p=128)
                wkv_v = self.w_ukv.rearrange("(k p) e -> p k e", p=128)
                allcq = cq_res
                allckv = ckv_res
                nst = 0
                for h in range(16):
                    wq, wq_r, wq_s = wq_ring.next()
                    fw.dma(fw.pool, wq_s, [(wq[:], wq_v[:, :, h * 192:(h + 1) * 192])], writes=[wq_r])
                    wkv, wkv_r, wkv_s = wkv_ring.next()
                    fw.dma(fw.pool, wkv_s, [(wkv[:], wkv_v[:, :, h * 256:(h + 1) * 256])], writes=[wkv_r])
                    qn, qn_r, _ = qn_ring.next()
                    qr, qr_r, _ = qr_ring.next()
                    kn, kn_r, _ = kn_ring.next()
                    vh, vh_r, _ = v_ring.next()
                    nb = 0
                    for tb in range(4):
                        ts = slice(tb * 512, (tb + 1) * 512)
                        for what in range(3):
                            bank = nb % 2
                            nb += 1
                            pf = self.pf[bank]
                            if what == 0:
                                Dh, wsl, src, srcr = 128, wq[:, :, 0:128], cqT, allcq[tb]
                            elif what == 1:
                                Dh, wsl, src, srcr = 64, wq[:, :, 128:192], cqT, allcq[tb]
                            else:
                                Dh, wsl, src, srcr = 128, wkv[:, :, 0:128], ckvT, allckv[tb]

                            def mm():
                                ins = None
                                for k in range(4):
                                    ins = nc.tensor.matmul(pf[0:Dh, :], wsl[:, k, :], src[:, k, ts],
                                                           start=(k == 0), stop=(k == 3))
                                return ins
                            fw.op(fw.pe, mm, reads=[wq_r if what < 2 else wkv_r, srcr], writes=[self.pr[bank]])
                            if what == 0:
                                self.qk_post(128, bank, cols[:, 8:9], qn[:, ts], qn_r)
                            elif what == 1:
                                self.qk_post(64, bank, cols[0:64, 9:10], qr[:, ts], qr_r,
                                             rope=(cos64[:, ts], sin64[:, ts], self.perm64, tab_res))
                            else:
                                self.qk_post(128, bank, cols[:, 10:11], kn[:, ts], kn_r)
                    self.flush_post()
                    for tg in range(4):
                        bank = nb % 2
                        nb += 1
                        pf = self.pf[bank]

                        def mmv():
                            ins = None
                            for j in range(4):
                                tt = tg * 4 + j
                                for k in range(4):
                                    ins = nc.tensor.matmul(pf[:, j * 128:(j + 1) * 128],
                                                           ckvT[:, k, tt * 128:(tt + 1) * 128], wkv[:, k, 128:256],
                                                           start=(j == 0 and k == 0), stop=(k == 3),
                                                           skip_group_check=True)
                            return ins
                        fw.op(fw.pe, mmv, reads=[wkv_r, allckv[tg]], writes=[self.pr[bank]])
                        fw.op(fw.act, lambda: nc.scalar.copy(vh[:, tg * 4:(tg + 1) * 4, :],
                                                             pf[:, :].rearrange("p (j d) -> p j d", j=4)),
                              reads=[self.pr[bank]], writes=[vh_r])
                    oh, oh_r, oh_s = oh_ring.next()
                    for qb in range(4):
                        qs0 = qb * 512
                        ob, lb = 4, 5
                        nk = 4 * qb + 4
                        pend_pv = []
                        for j in range(nk):
                            r = j - 4 * qb
                            c0 = 128 * r if r > 0 else 0
                            sb = ST_BANKS[nst % len(ST_BANKS)]
                            nst += 1
                            ks = slice(j * 128, (j + 1) * 128)
                            qs = slice(qs0 + c0, qs0 + 512)

                            def mms():
                                nc.tensor.matmul(self.pf[sb][:, c0:512], kn[:, ks], qn[:, qs], start=True, stop=False)
                                return nc.tensor.matmul(self.pf[sb][:, c0:512], krT[:, ks], qr[:, qs],
                                                        start=False, stop=True)
                            fw.op(fw.pe, mms, reads=[kn_r, qn_r, qr_r] + kr_res, writes=[self.pr[sb]])
                            pt, pt_r, _ = pt_ring.next()
                            fw.op(fw.act, lambda: nc.scalar.activation(pt[:, c0:512], self.pf[sb][:, c0:512], AF.Exp,
                                                                       bias=self.zero_col, scale=scale),
                                  reads=[self.pr[sb], self.c_res], writes=[pt_r])
                            if r >= 0:
                                fw.op(fw.dve, lambda: nc.vector.tensor_tensor(pt[:, c0:c0 + 128], pt[:, c0:c0 + 128],
                                                                              self.mask_cur, ALU.mult),
                                      reads=[pt_r, self.c_res], writes=[pt_r])

                            def mk_pv(j=j, c0=c0, pt=pt, pt_r=pt_r):
                                def mmo():
                                    nc.tensor.matmul(self.pf[ob][:, c0:512], vh[:, j, :], pt[:, c0:512],
                                                     start=(j == 0), stop=(j == nk - 1), skip_group_check=True)
                                    return nc.tensor.matmul(self.pf[lb][:, c0:512], self.ones, pt[:, c0:512],
                                                            start=(j == 0), stop=(j == nk - 1),
                                                            skip_group_check=True)
                                return lambda: fw.op(fw.pe, mmo, reads=[vh_r, pt_r, self.c_res],
                                                     writes=[self.pr[ob], self.pr[lb]])
                            pend_pv.append(mk_pv())
                            while len(pend_pv) > LOOKAHEAD:
                                pend_pv.pop(0)()
                        while pend_pv:
                            pend_pv.pop(0)()
                        rl, rl_r, _ = rl_ring.next()
                        fw.op(fw.dve, lambda: nc.vector.reciprocal(rl[:], self.pf[lb][:, :]), reads=[self.pr[lb]],
                              writes=[rl_r])
                        fw.op(fw.dve, lambda: nc.vector.tensor_tensor(oh[:, qs0:qs0 + 512], self.pf[ob][:, :], rl[:],
                                                                      ALU.mult),
                              reads=[self.pr[ob], rl_r], writes=[oh_r])
                    fw.dma(fw.sp, oh_s, [(self.ot_scr[h], oh[:])], reads=[oh_r])
                fw.barrier()
        if self.ck("m2"):
            return
        self.wo_phase(self.mla_w_o, x_in, x_out)

    def dil_mixer(self, x_in, x_out):
        nc, fw = self.nc, self.fw
        cols = self.cols
        DIL = (1, 4, 16)
        scale = 1.0 / math.sqrt(128.0)
        with ExitStack() as st:
            cos128 = st.enter_context(nc.sbuf_tensor("cos128", [128, S], BF16))
            sin128 = st.enter_context(nc.sbuf_tensor("sin128", [128, S], BF16))
            tab_res = Res()
            self.rope_tables(128, cols[:, 18:19], cols[:, 19:20], cos128, sin128, tab_res)
            hT = st.enter_context(nc.sbuf_tensor("hT_d", [128, 16, S], BF16))
            hT_res = [[Res(), Res()] for _ in range(16)]
            w_ring = Ring(self, st, "wqkv", 3, [128, 16, 384], BF16, dma=True)
            wv = self.w_qkv.rearrange("(k p) e -> p k e", p=128)

            def load_w(hh):
                wts = []
                for c in range(3):
                    wt, wt_r, wt_s = w_ring.next()
                    e0 = (hh * 9 + c * 3) * 128
                    fw.dma(fw.pool, wt_s, [(wt[:, j * 8:(j + 1) * 8, :], wv[:, j * 8:(j + 1) * 8, e0:e0 + 384])
                                           for j in range(2)], writes=[wt_r])
                    wts.append((wt, wt_r))
                return wts
            pre_wts = load_w(0)
            self.norm_to_hT(st, x_in, self.mixer_norm[1:2, :], hT, hT_res, 0, 16)
            all_h = [r for tr in hT_res for r in tr]
            self.make_post(st)
            qk = [[st.enter_context(nc.sbuf_tensor("qk%d%d" % (c, g), [128, S], BF16)) for g in range(3)]
                  for c in range(2)]
            qk_res = [[Res() for g in range(3)] for c in range(2)]
            vv = [st.enter_context(nc.sbuf_tensor("vv%d" % g, [128, 16, 128], BF16)) for g in range(3)]
            vv_res = [Res() for g in range(3)]
            accO = st.enter_context(nc.sbuf_tensor("accO", [128, S], F32))
            accL = st.enter_context(nc.sbuf_tensor("accL", [128, S], F32))
            acc_res = Res()
            pt_ring = Ring(self, st, "ptd", LOOKAHEAD + 2, [128, 256], BF16)
            oh_ring = Ring(self, st, "ohd", 2, [128, S], BF16, dma=True)
            nst = 0
            pend_final = None
            for hh in range(16):
                wts = pre_wts if hh == 0 else load_w(hh)
                nb = 0
                for c in range(2):
                    wt, wt_r = wts[c]
                    for g in range(3):
                        for tb in range(4):
                            ts = slice(tb * 512, (tb + 1) * 512)
                            hr = [r for ti in range(tb * 4, tb * 4 + 4) for r in hT_res[ti]]
                            bank = nb % 2
                            nb += 1
                            pf = self.pf[bank]

                            def mm():
                                ins = None
                                for k in range(16):
                                    ins = nc.tensor.matmul(pf[:, :], wt[:, k, g * 128:(g + 1) * 128], hT[:, k, ts],
                                                           start=(k == 0), stop=(k == 15))
                                return ins
                            fw.op(fw.pe, mm, reads=[wt_r] + hr, writes=[self.pr[bank]])
                            self.qk_post(128, bank, cols[:, 12 + c * 3 + g:13 + c * 3 + g], qk[c][g][:, ts],
                                         qk_res[c][g],
                                         rope=(cos128[:, ts], sin128[:, ts], self.perm128, tab_res))
                            if nb == 3 and pend_final is not None:
                                pend_final()
                                pend_final = None
                self.flush_post()
                wt, wt_r = wts[2]
                for g in range(3):
                    dil = DIL[g]
                    nlb = 16 // dil
                    for cg in range(4):
                        bank = nb % 2
                        nb += 1
                        pf = self.pf[bank]

                        def mmv():
                            ins = None
                            for j in range(4):
                                ci = cg * 4 + j
                                r, lbk = ci // nlb, ci % nlb
                                t_lo = lbk * 128 * dil + r
                                tsl = slice(t_lo, t_lo + 127 * dil + 1, dil)
                                for k in range(16):
                                    ins = nc.tensor.matmul(pf[:, j * 128:(j + 1) * 128], hT[:, k, tsl],
                                                           wt[:, k, g * 128:(g + 1) * 128],
                                                           start=(j == 0 and k == 0), stop=(k == 15),
                                                           skip_group_check=True)
                            return ins
                        fw.op(fw.pe, mmv, reads=[wt_r] + all_h, writes=[self.pr[bank]])
                        fw.op(fw.act, lambda: nc.scalar.copy(vv[g][:, cg * 4:(cg + 1) * 4, :],
                                                             pf[:, :].rearrange("p (j d) -> p j d", j=4)),
                              reads=[self.pr[bank]], writes=[vv_res[g]])
                for g in range(3):
                    dil = DIL[g]
                    nlb = 16 // dil
                    qT, kT = qk[0][g], qk[1][g]
                    started = {}

                    def pv(ci_q, pt_ap, kchunk, pt_r):
                        sbk = ci_q // 4
                        ob = 4 + (sbk % 2)
                        lb_ = 6 + (sbk % 2)
                        cc = (ci_q % 4) * 128
                        first = not started.get(sbk, False)
                        started[sbk] = True

                        def f():
                            nc.tensor.matmul(self.pf[ob][:, cc:cc + 128], vv[g][:, kchunk, :], pt_ap,
                                             start=first, stop=False, skip_group_check=True)
                            return nc.tensor.matmul(self.pf[lb_][:, cc:cc + 128], self.ones, pt_ap,
                                                    start=first, stop=False, skip_group_check=True)
                        fw.op(fw.pe, f, reads=[vv_res[g], pt_r, self.c_res], writes=[self.pr[ob], self.pr[lb_]])

                    def flush(sbk):
                        ob = 4 + (sbk % 2)
                        lb_ = 6 + (sbk % 2)
                        if dil == 1:
                            oa = accO[:, sbk * 512:(sbk + 1) * 512]
                            la = accL[:, sbk * 512:(sbk + 1) * 512]
                            po = self.pf[ob][:, :]
                            pl = self.pf[lb_][:, :]
                        elif dil == 4:
                            oa = accO[:, sbk:S:4]
                            la = accL[:, sbk:S:4]
                            po = self.pf[ob][:, :]
                            pl = self.pf[lb_][:, :]
                        else:
                            oa = accO[:].rearrange("p (a r) -> p r a", r=16)[:, sbk * 4:(sbk + 1) * 4, :]
                            la = accL[:].rearrange("p (a r) -> p r a", r=16)[:, sbk * 4:(sbk + 1) * 4, :]
                            po = self.pf[ob][:, :].rearrange("p (r a) -> p r a", r=4)
                            pl = self.pf[lb_][:, :].rearrange("p (r a) -> p r a", r=4)
                        if g == 0:
                            fw.op(fw.dve, lambda: nc.vector.tensor_copy(oa, po), reads=[self.pr[ob]],
                                  writes=[acc_res])
                            fw.op(fw.act, lambda: nc.scalar.copy(la, pl), reads=[self.pr[lb_]], writes=[acc_res])
                        else:
                            fw.op(fw.dve, lambda: nc.vector.tensor_tensor(oa, po, oa, ALU.add),
                                  reads=[self.pr[ob], acc_res], writes=[acc_res])
                            fw.op(fw.dve, lambda: nc.vector.tensor_tensor(la, pl, la, ALU.add),
                                  reads=[self.pr[lb_], acc_res], writes=[acc_res])

                    pend_tail = []
                    for ci in range(16):
                        r, lbk = ci // nlb, ci % nlb
                        has_next = lbk + 1 < nlb
                        nq = 256 if has_next else 128
                        t_lo = lbk * 128 * dil + r
                        ksl = slice(t_lo, t_lo + 127 * dil + 1, dil)
                        qsl = slice(t_lo, t_lo + (nq - 1) * dil + 1, dil)
                        sb = ST_BANKS[nst % len(ST_BANKS)]
                        nst += 1
                        fw.op(fw.pe, lambda: nc.tensor.matmul(self.pf[sb][:, 0:nq], kT[:, ksl], qT[:, qsl],
                                                              start=True, stop=True),
                              reads=[qk_res[0][g], qk_res[1][g]], writes=[self.pr[sb]])
                        pt, pt_r, _ = pt_ring.next()
                        fw.op(fw.act, lambda: nc.scalar.activation(pt[:, 0:nq], self.pf[sb][:, 0:nq], AF.Exp,
                                                                   bias=self.zero_col, scale=scale),
                              reads=[self.pr[sb], self.c_res], writes=[pt_r])
                        fw.op(fw.dve, lambda: nc.vector.tensor_tensor(pt[:, 0:nq], pt[:, 0:nq], self.mask2[:, 0:nq],
                                                                      ALU.mult),
                              reads=[pt_r, self.c_res], writes=[pt_r])
                        def mk_tail(ci=ci, pt=pt, pt_r=pt_r, has_next=has_next):
                            def tail():
                                pv(ci, pt[:, 0:128], ci, pt_r)
                                if has_next:
                                    pv(ci + 1, pt[:, 128:256], ci, pt_r)
                                if ci % 4 == 3:
                                    flush(ci // 4)
                            return tail
                        pend_tail.append(mk_tail())
                        while len(pend_tail) > LOOKAHEAD:
                            pend_tail.pop(0)()
                    while pend_tail:
                        pend_tail.pop(0)()
                def mk_final(hh=hh):
                    def fin():
                        oh, oh_r, oh_s = oh_ring.next()
                        fw.op(fw.dve, lambda: nc.vector.reciprocal(accL[:], accL[:]), reads=[acc_res],
                              writes=[acc_res])
                        fw.op(fw.dve, lambda: nc.vector.tensor_tensor(oh[:], accO[:], accL[:], ALU.mult),
                              reads=[acc_res], writes=[oh_r])
                        fw.dma(fw.sp, oh_s, [(self.ot_scr[hh], oh[:])], reads=[oh_r])
                    return fin
                pend_final = mk_final()
            if pend_final is not None:
                pend_final()
                pend_final = None
            fw.barrier()
        self.wo_phase(self.dil_w_o, x_in, x_out)


def _consts():
    bf = ml_dtypes.bfloat16
    cb = np.zeros((128, 768), np.float32)
    cb[:, 0:128] = np.eye(128)
    cb[:, 128:256] = 1.0
    dp = np.arange(128)
    perm = np.zeros((128, 128), np.float32)
    perm[(dp + 64) % 128, dp] = 1.0
    cb[:, 256:384] = perm
    dp = np.arange(64)
    perm64 = np.zeros((64, 64), np.float32)
    perm64[(dp + 32) % 64, dp] = 1.0
    cb[0:64, 384:448] = perm64
    a = np.arange(128)[:, None]
    b = np.arange(128)[None, :]
    cb[:, 512:640] = (a <= b)
    cb[:, 640:768] = (a >= b)
    return cb.astype(bf)


def _cols(inp):
    c = np.zeros((128, 32), np.float32)
    c[:, 0:4] = inp["mla_q_norm"][0].reshape(4, 128).T
    c[:, 4:8] = inp["mla_kv_norm"][0].reshape(4, 128).T
    c[:, 8] = inp["mla_q_gain"][0][:128]
    c[:64, 9] = inp["mla_q_gain"][0][128:]
    c[:, 10] = inp["mla_k_gain"][0][:128]
    c[:64, 11] = inp["mla_k_gain"][0][128:]
    c[:, 12:15] = inp["dil_q_gain"][0].T
    c[:, 15:18] = inp["dil_k_gain"][0].T
    p = np.arange(128)
    c[:, 18] = np.power(np.float32(10000.0), (-2.0 * (p % 64).astype(np.float32) / np.float32(128.0))).astype(np.float32)
    c[:, 19] = np.where(p < 64, -1.0, 1.0)
    p = np.arange(64)
    c[:64, 20] = np.power(np.float32(10000.0), (-2.0 * (p % 32).astype(np.float32) / np.float32(64.0))).astype(np.float32)
    c[:64, 21] = np.where(p < 32, -1.0, 1.0)
    return c


def make_in_maps(inp, cores):
    cb = _consts()
    cols = _cols(inp)
    f = lambda a: np.ascontiguousarray(np.asarray(a, dtype=np.float32))
    wqkv_r = np.ascontiguousarray(
        np.asarray(inp["dil_w_qkv"][0], np.float32).reshape(D, 9, 16, 128).transpose(0, 2, 1, 3).reshape(D, 18432))
    shared = {
        "mixer_norm": f(inp["mixer_norm"]), "ffn_norm": f(inp["ffn_norm"]),
        "mla_w_down": f(inp["mla_w_down"][0]), "mla_w_uq": f(inp["mla_w_uq"][0]),
        "mla_w_ukv": f(inp["mla_w_ukv"][0]), "mla_w_o": f(inp["mla_w_o"][0]),
        "dil_w_qkv_r": wqkv_r, "dil_w_o": f(inp["dil_w_o"][0]),
        "ffn_w_gate": f(inp["ffn_w_gate"]), "ffn_w_up": f(inp["ffn_w_up"]), "ffn_w_down": f(inp["ffn_w_down"]),
        "cols": cols, "cb": cb,
    }
    maps = []
    for b in cores:
        m = dict(shared)
        m["x"] = f(inp["x"][b])
        m["pos"] = np.ascontiguousarray(np.asarray(inp["positions"][b], np.int32).reshape(1, S))
        maps.append(m)
    return maps


def kernel(**inputs):
    prog = Prog()
    maps = make_in_maps(inputs, list(range(8)))
    res = run_bass_kernel_spmd(prog.nc, maps, core_ids=list(range(8)))
    return np.stack([np.asarray(r["out"], np.float32) for r in res.results], axis=0)
```

```python
import math
from contextlib import ExitStack
import numpy as np
import ml_dtypes
import concourse.bass as bass
import concourse.mybir as mybir
from concourse.bass_utils import run_bass_kernel_spmd

F32 = mybir.dt.float32
BF16 = mybir.dt.bfloat16
I32 = mybir.dt.int32
AF = mybir.ActivationFunctionType
ALU = mybir.AluOpType

SAME_ENGINE_SYNC = True
LOOKAHEAD = 3
POST_DEFER = 3
ST_BANKS = (2, 3, 0, 1)
NOSYNC = ()
D = 2048
S = 2048
FF = 5632
EPS = 1e-6
PI = math.pi
TWO_PI = 2.0 * math.pi
C1 = 6.28125
C2 = TWO_PI - C1


class Sem:
    def __init__(self, nc, name):
        self.h = nc.alloc_semaphore(name)
        self.count = 0


class Res:
    __slots__ = ("last_w", "readers")

    def __init__(self):
        self.last_w = None
        self.readers = []


class Eng:
    def __init__(self, nc, name, eng):
        self.name = name
        self.eng = eng
        self.sem = Sem(nc, "s_" + name)
        self.seen = {}
        self.is_pe = name == "pe"


class FW:
    def __init__(self, nc):
        self.nc = nc
        self.pe = Eng(nc, "pe", nc.tensor)
        self.act = Eng(nc, "act", nc.scalar)
        self.dve = Eng(nc, "dve", nc.vector)
        self.pool = Eng(nc, "pool", nc.gpsimd)
        self.sp = Eng(nc, "sp", nc.sync)
        self.engs = [self.pe, self.act, self.dve, self.pool, self.sp]
        self.dma_sems = {}
        self.n_inst = 0
        self._uid = 0

    def uid(self):
        self._uid += 1
        return self._uid

    def dsem(self, name):
        if name not in self.dma_sems:
            self.dma_sems[name] = Sem(self.nc, "d_" + name)
        return self.dma_sems[name]

    def _waits(self, E, reads, writes):
        deps = {}
        for r in reads:
            if r.last_w is not None:
                s, v = r.last_w
                if deps.get(s, 0) < v:
                    deps[s] = v
        for w in writes:
            if w.last_w is not None:
                s, v = w.last_w
                if deps.get(s, 0) < v:
                    deps[s] = v
            for (s, v) in w.readers:
                if deps.get(s, 0) < v:
                    deps[s] = v
        for s, v in deps.items():
            if s is E.sem and (E.is_pe or not SAME_ENGINE_SYNC or E.name in NOSYNC):
                continue
            if E.seen.get(s, 0) >= v:
                continue
            E.eng.wait_ge(s.h, v)
            self.n_inst += 1
            E.seen[s] = v

    def _commit(self, tag, reads, writes):
        for r in reads:
            r.readers.append(tag)
        for w in writes:
            w.last_w = tag
            w.readers = []

    def op(self, E, fn, reads=(), writes=()):
        self._waits(E, reads, writes)
        inst = fn()
        E.sem.count += 1
        inst.then_inc(E.sem.h, 1)
        self.n_inst += 1
        tag = (E.sem, E.sem.count)
        self._commit(tag, reads, writes)
        return tag

    def dma(self, E, sem, pairs, reads=(), writes=()):
        self._waits(E, reads, writes)
        for (o, i) in pairs:
            E.eng.dma_start(out=o, in_=i).then_inc(sem.h, 16)
            sem.count += 16
            self.n_inst += 1
        tag = (sem, sem.count)
        self._commit(tag, reads, writes)
        return tag

    def barrier(self):
        sems = [e.sem for e in self.engs] + list(self.dma_sems.values())
        for E in self.engs:
            for s in sems:
                if s is E.sem:
                    continue
                if s.count > 0 and E.seen.get(s, 0) < s.count:
                    E.eng.wait_ge(s.h, s.count)
                    E.seen[s] = s.count
                    self.n_inst += 1


class Ring:
    def __init__(self, K, stack, name, n, shape, dtype, dma=False):
        self.t = [stack.enter_context(K.nc.sbuf_tensor("%s%d_%d" % (name, i, K.fw.uid()), shape, dtype))
                  for i in range(n)]
        self.r = [Res() for _ in range(n)]
        self.s = [K.fw.dsem("%s%d" % (name, i)) for i in range(n)] if dma else [None] * n
        self.i = -1
        self.n = n

    def next(self):
        self.i = (self.i + 1) % self.n
        return self.t[self.i], self.r[self.i], self.s[self.i]


class Stop(Exception):
    pass


class Prog:
    def ck(self, name):
        if self.stop == name:
            self.fw.barrier()
            self.stopped = True
            return True
        return False

    def __init__(self, debug=False, stop=None, lite=False, skip=()):
        self.debug = debug
        self.stop = stop
        self.stopped = False
        nc = self.nc = bass.Bass("TRN2", target_bir_lowering=False)
        self.fw = FW(nc)
        fw = self.fw

        self.skip = skip

        def din(name, shape, dt=F32):
            if lite and name not in ("x", "pos", "mixer_norm", "ffn_norm", "cols", "cb") + tuple(lite):
                shape = [2, 2]
            return nc.dram_tensor(name, shape, dt, kind="ExternalInput").ap()

        self.x = din("x", [S, D])
        self.pos = din("pos", [1, S], I32)
        self.mixer_norm = din("mixer_norm", [2, D])
        self.ffn_norm = din("ffn_norm", [2, D])
        self.w_down = din("mla_w_down", [D, 1088])
        self.w_uq = din("mla_w_uq", [512, 3072])
        self.w_ukv = din("mla_w_ukv", [512, 4096])
        self.mla_w_o = din("mla_w_o", [D, D])
        self.w_qkv = din("dil_w_qkv_r", [D, 16 * 9 * 128])
        self.dil_w_o = din("dil_w_o", [D, D])
        self.w_gate = din("ffn_w_gate", [2, D, FF])
        self.w_up = din("ffn_w_up", [2, D, FF])
        self.w_dn = din("ffn_w_down", [2, FF, D])
        self.cols_d = din("cols", [128, 32])
        self.cb_d = din("cb", [128, 768], BF16)
        kind = "ExternalOutput" if debug else "Internal"
        self.x1 = nc.dram_tensor("x1", [S, D], F32, kind=kind).ap()
        self.x2 = nc.dram_tensor("x2", [S, D], F32, kind=kind).ap()
        self.x3 = nc.dram_tensor("x3", [S, D], F32, kind=kind).ap()
        self.out = nc.dram_tensor("out", [S, D], F32, kind="ExternalOutput").ap()
        self.ot_scr = nc.dram_tensor("ot_scr", [16, 128, S], BF16, kind=kind).ap()

        self.pb = [nc.alloc_psum_tensor("pb%d" % i, [128, 1024], BF16) for i in range(8)]
        self.pf = [b.bitcast(F32) for b in self.pb]
        self.pr = [Res() for _ in range(8)]

        with ExitStack() as st:
            self.cols = st.enter_context(nc.sbuf_tensor("cols_sb", [128, 32], F32))
            self.cb = st.enter_context(nc.sbuf_tensor("cb_sb", [128, 768], BF16))
            self.cst = st.enter_context(nc.sbuf_tensor("cst", [128, 4], F32))
            self.c_res = Res()
            fw.dma(fw.sp, fw.dsem("c0"), [(self.cols[:], self.cols_d)], writes=[self.c_res])
            fw.dma(fw.sp, fw.dsem("c1"), [(self.cb[:], self.cb_d)], writes=[self.c_res])
            fw.op(fw.dve, lambda: nc.vector.memset(self.cst[:, 0:1], EPS), writes=[self.c_res])
            fw.op(fw.dve, lambda: nc.vector.memset(self.cst[:, 1:2], 0.0), writes=[self.c_res])
            self.ident = self.cb[:, 0:128]
            self.ones = self.cb[:, 128:256]
            self.perm128 = self.cb[:, 256:384]
            self.perm64 = self.cb[0:64, 384:448]
            self.mask_cur = self.cb[:, 512:640]
            self.mask2 = self.cb[:, 512:768]
            self.eps_col = self.cst[:, 0:1]
            self.zero_col = self.cst[:, 1:2]
            fw.barrier()
            try:
                self.body()
            except Stop:
                pass
            fw.barrier()

    def body(self):
        stop = self.stop
        self.mla_mixer(self.x, self.x1)
        if stop == "mla" or self.stopped:
            return
        self.ffn(0, self.x1, self.x2)
        if stop == "ffn0":
            return
        self.dil_mixer(self.x2, self.x3)
        if stop == "dil" or self.stopped:
            return
        self.ffn(1, self.x3, self.out)

    def norm_to_hT(self, st_outer, x_dram, gain_row, hT, hT_res, t0, ntile):
        nc, fw = self.nc, self.fw
        with ExitStack() as st:
            gain = st.enter_context(nc.sbuf_tensor("gain_%d" % fw.uid(), [128, D], F32))
            g_res = Res()
            fw.dma(fw.sp, fw.dsem("gain"), [(gain[:], gain_row.partition_broadcast(128))], writes=[g_res])
            xt_ring = Ring(self, st, "xt", 2, [128, D], F32, dma=True)
            xn_ring = Ring(self, st, "xn", 2, [128, D], BF16)
            junk_ring = Ring(self, st, "junk", 1, [128, D], BF16)
            ss_ring = Ring(self, st, "nss", 2, [128, 4], F32)
            for ti in range(ntile):
                tok = t0 + ti * 128
                xt, xt_r, xt_s = xt_ring.next()
                fw.dma(fw.sp, xt_s, [(xt[:], x_dram[tok:tok + 128, :])], writes=[xt_r])
                jk, jk_r, _ = junk_ring.next()
                ss, ss_r, _ = ss_ring.next()
                fw.op(fw.act, lambda: nc.scalar.activation(jk[:], xt[:], AF.Square, accum_out=ss[:, 0:1]),
                      reads=[xt_r], writes=[jk_r, ss_r])
                fw.op(fw.act, lambda: nc.scalar.activation(ss[:, 1:2], ss[:, 0:1], AF.Ln, bias=self.eps_col,
                                                           scale=1.0 / D),
                      reads=[ss_r, self.c_res], writes=[ss_r])
                fw.op(fw.act, lambda: nc.scalar.activation(ss[:, 2:3], ss[:, 1:2], AF.Exp, bias=self.zero_col,
                                                           scale=-0.5),
                      reads=[ss_r], writes=[ss_r])
                xn, xn_r, _ = xn_ring.next()
                fw.op(fw.dve, lambda: nc.vector.scalar_tensor_tensor(xn[:], xt[:], ss[:, 2:3], gain[:],
                                                                     ALU.mult, ALU.mult),
                      reads=[xt_r, ss_r, g_res], writes=[xn_r])
                for half in range(2):
                    bank = half
                    pbk = self.pb[bank]

                    def tr():
                        ins = None
                        for j in range(8):
                            k = half * 8 + j
                            ins = nc.tensor.transpose(pbk[:, j * 128:(j + 1) * 128], xn[:, k * 128:(k + 1) * 128],
                                                      self.ident)
                        return ins
                    fw.op(fw.pe, tr, reads=[xn_r, self.c_res], writes=[self.pr[bank]])
                    src = pbk[:].rearrange("p (k t) -> p k t", k=8)
                    dst = hT[:, half * 8:(half + 1) * 8, ti * 128:(ti + 1) * 128]
                    if half == 0:
                        fw.op(fw.act, lambda: nc.scalar.copy(dst, src), reads=[self.pr[bank]],
                              writes=[hT_res[ti][half]])
                    else:
                        fw.op(fw.dve, lambda: nc.vector.tensor_copy(dst, src), reads=[self.pr[bank]],
                              writes=[hT_res[ti][half]])
            fw.barrier()

    def proj_residual(self, aT, aT_reads, nk, ntile, w_dram, res_dram, out_dram, t0, ncol_blk, wname):
        nc, fw = self.nc, self.fw
        cw = D // ncol_blk
        with ExitStack() as st:
            nsplit = 4 if nk >= 16 else 1
            kk = nk // nsplit
            w_rings = [Ring(self, st, "%sq%d" % (wname, j), 2, [128, kk, cw], BF16, dma=True) for j in range(nsplit)]
            xr_ring = Ring(self, st, "xr", 3, [128, cw], F32, dma=True)
            o_ring = Ring(self, st, "ost", 3, [128, cw], F32, dma=True)
            wv = w_dram.rearrange("(k p) d -> p k d", p=128)
            nb = 0
            for cb_i in range(ncol_blk):
                c0 = cb_i * cw
                parts = []
                for j in range(nsplit):
                    wt, wt_r, wt_s = w_rings[j].next()
                    fw.dma(fw.pool, wt_s, [(wt[:], wv[:, j * kk:(j + 1) * kk, c0:c0 + cw])], writes=[wt_r])
                    parts.append((wt, wt_r))
                for ti in range(ntile):
                    tok = t0 + ti * 128
                    bank = 2 + (nb % 2)
                    nb += 1
                    pf = self.pf[bank]
                    for j in range(nsplit):
                        wt, wt_r = parts[j]

                        def mm():
                            ins = None
                            for kq in range(kk):
                                k = j * kk + kq
                                ins = nc.tensor.matmul(pf[:, 0:cw], aT[:, k, ti * 128:(ti + 1) * 128], wt[:, kq, :],
                                                       start=(k == 0), stop=(k == nk - 1))
                            return ins
                        fw.op(fw.pe, mm, reads=[wt_r] + aT_reads(ti), writes=[self.pr[bank]])
                    xr, xr_r, xr_s = xr_ring.next()
                    fw.dma(fw.sp, xr_s, [(xr[:], res_dram[tok:tok + 128, c0:c0 + cw])], writes=[xr_r])
                    ot, ot_r, ot_s = o_ring.next()
                    fw.op(fw.dve, lambda: nc.vector.tensor_tensor(ot[:], pf[:, 0:cw], xr[:], ALU.add),
                          reads=[self.pr[bank], xr_r], writes=[ot_r])
                    fw.dma(fw.sp, ot_s, [(out_dram[tok:tok + 128, c0:c0 + cw], ot[:])], reads=[ot_r])
            fw.barrier()

    def ffn(self, layer, x_in, x_out):
        nc, fw = self.nc, self.fw
        NT = 1024
        for half in range(2):
            t0 = half * NT
            with ExitStack() as st:
                aT = st.enter_context(nc.sbuf_tensor("aT_%d" % fw.uid(), [128, 44, NT], BF16))
                aT_res = [[Res() for _ in range(2)] for _ in range(44)]
                with ExitStack() as st2:
                    hT = st2.enter_context(nc.sbuf_tensor("hTf_%d" % fw.uid(), [128, 16, NT], BF16))
                    hT_res = [[Res(), Res()] for _ in range(NT // 128)]
                    self.norm_to_hT(st2, x_in, self.ffn_norm[layer:layer + 1, :], hT, hT_res, t0, NT // 128)
                    w_ring = Ring(self, st2, "wgu", 3, [128, 2, 16, 256], BF16, dma=True)
                    sg_ring = Ring(self, st2, "sg", 2, [128, 512], F32)
                    wg_v = self.w_gate[layer].rearrange("(k p) f -> p k f", p=128)
                    wu_v = self.w_up[layer].rearrange("(k p) f -> p k f", p=128)
                    nb = 0
                    for fc2 in range(22):
                        wt, wt_r, wt_s = w_ring.next()
                        f0 = fc2 * 256
                        fw.dma(fw.pool, wt_s, [(wt[:, 0], wg_v[:, :, f0:f0 + 256]),
                                               (wt[:, 1], wu_v[:, :, f0:f0 + 256])], writes=[wt_r])
                        for sub in range(2):
                            fc = fc2 * 2 + sub
                            for tb in range(NT // 512):
                                hr = [r for ti in range(tb * 4, tb * 4 + 4) for r in hT_res[ti]]
                                banks = [2 + (nb % 2) * 2, 3 + (nb % 2) * 2]
                                nb += 1
                                for gu in range(2):
                                    pf = self.pf[banks[gu]]

                                    def mm():
                                        ins = None
                                        for k in range(16):
                                            ins = nc.tensor.matmul(pf[:, :], wt[:, gu, k, sub * 128:(sub + 1) * 128],
                                                                   hT[:, k, tb * 512:(tb + 1) * 512],
                                                                   start=(k == 0), stop=(k == 15))
                                        return ins
                                    fw.op(fw.pe, mm, reads=[wt_r] + hr, writes=[self.pr[banks[gu]]])
                                sg, sg_r, _ = sg_ring.next()
                                fw.op(fw.act, lambda: nc.scalar.activation(sg[:], self.pf[banks[0]][:, :], AF.Silu),
                                      reads=[self.pr[banks[0]]], writes=[sg_r])
                                fw.op(fw.dve, lambda: nc.vector.tensor_tensor(aT[:, fc, tb * 512:(tb + 1) * 512],
                                                                              self.pf[banks[1]][:, :], sg[:],
                                                                              ALU.mult),
                                      reads=[self.pr[banks[1]], sg_r], writes=[aT_res[fc][tb]])
                    fw.barrier()
                allr = [r for fr in aT_res for r in fr]
                self.proj_residual(aT, lambda ti: allr, 44, NT // 128, self.w_dn[layer], x_in, x_out, t0, 4, "wdn")

    def rope_tables(self, Dh, invf_col, sgn_col, cosT, sinS, tab_res):
        nc, fw = self.nc, self.fw
        with ExitStack() as st:
            posi = st.enter_context(nc.sbuf_tensor("posi_%d" % fw.uid(), [128, S], I32))
            posf = st.enter_context(nc.sbuf_tensor("posf_%d" % fw.uid(), [128, S], F32))
            ang = st.enter_context(nc.sbuf_tensor("ang_%d" % fw.uid(), [128, 512], F32))
            v = st.enter_context(nc.sbuf_tensor("v_%d" % fw.uid(), [128, 512], F32))
            ki = st.enter_context(nc.sbuf_tensor("ki_%d" % fw.uid(), [128, 512], I32))
            kf = st.enter_context(nc.sbuf_tensor("kf_%d" % fw.uid(), [128, 512], F32))
            m = st.enter_context(nc.sbuf_tensor("m_%d" % fw.uid(), [128, 512], F32))
            R = Res()
            fw.dma(fw.sp, fw.dsem("posi"), [(posi[:], self.pos.partition_broadcast(128))], writes=[R])
            V = nc.vector

            def dv(fn):
                fw.op(fw.dve, fn, reads=[R, self.c_res], writes=[R])
            if "conv0" in self.skip:
                dv(lambda: V.memset(posf[:], 3.0))
            else:
                dv(lambda: V.tensor_copy(posf[:], posi[:]))
            P = slice(0, Dh)
            for tb in range(4 if "short" not in self.skip else 1):
                cs = slice(tb * 512, (tb + 1) * 512)
                for which in range(2):
                    dv(lambda: V.tensor_scalar(ang[P], posf[P, cs], invf_col, None, ALU.mult))
                    if which == 1:
                        dv(lambda: V.tensor_scalar(ang[P], ang[P], PI / 2, None, ALU.add))
                    dv(lambda: V.tensor_scalar(v[P], ang[P], 1.0 / TWO_PI, None, ALU.mult))
                    if "conv" in self.skip:
                        dv(lambda: V.tensor_copy(kf[P], v[P]))
                    else:
                        dv(lambda: V.tensor_copy(ki[P], v[P]))
                        dv(lambda: V.tensor_copy(kf[P], ki[P]))
                    dv(lambda: V.scalar_tensor_tensor(ang[P], kf[P], -C1, ang[P], ALU.mult, ALU.add))
                    dv(lambda: V.scalar_tensor_tensor(ang[P], kf[P], -C2, ang[P], ALU.mult, ALU.add))
                    if "cmp" not in self.skip:
                        dv(lambda: V.tensor_scalar(m[P], ang[P], PI, -TWO_PI, ALU.is_gt, ALU.mult))
                        dv(lambda: V.tensor_tensor(ang[P], ang[P], m[P], ALU.add))
                        dv(lambda: V.tensor_scalar(m[P], ang[P], -PI, TWO_PI, ALU.is_lt, ALU.mult))
                        dv(lambda: V.tensor_tensor(ang[P], ang[P], m[P], ALU.add))
                    dv(lambda: V.tensor_scalar(ang[P], ang[P], 3.1415925, -3.1415925, ALU.min, ALU.max))
                    fw.op(fw.act, lambda: nc.scalar.activation(v[P], ang[P], AF.Sin if "sin" not in self.skip else AF.Copy,
                                                               bias=self.zero_col[P], scale=1.0),
                          reads=[R, self.c_res], writes=[R])
                    if which == 0:
                        dv(lambda: V.tensor_scalar(sinS[P, cs], v[P], sgn_col, None, ALU.mult))
                    else:
                        dv(lambda: V.tensor_copy(cosT[P, cs], v[P]))
            fw.op(fw.dve, lambda: V.tensor_copy(m[P, 0:1], m[P, 0:1]), reads=[R], writes=[R, tab_res])
            fw.barrier()

    def make_post(self, st):
        self.sq_ring = Ring(self, st, "sq", POST_DEFER + 1, [128, 512], BF16)
        self.qg_ring = Ring(self, st, "qg", POST_DEFER + 1, [128, 512], BF16)
        self.ln_ring = Ring(self, st, "lnv", 2, [128, 512], F32)
        self.rs_ring = Ring(self, st, "rstd", 2, [128, 512], F32)
        self.t1_ring = Ring(self, st, "t1", 2, [128, 512], F32)
        self.t2_ring = Ring(self, st, "t2", 2, [128, 512], F32)

    def rstd_from_ss(self, Dh, ss_bank, nfeat):
        nc, fw = self.nc, self.fw
        P = slice(0, Dh)
        ln, ln_r, _ = self.ln_ring.next()
        fw.op(fw.act, lambda: nc.scalar.activation(ln[P], self.pf[ss_bank][P, :], AF.Ln, bias=self.eps_col[P],
                                                   scale=1.0 / nfeat),
              reads=[self.pr[ss_bank], self.c_res], writes=[ln_r])
        rs, rs_r, _ = self.rs_ring.next()
        fw.op(fw.act, lambda: nc.scalar.activation(rs[P], ln[P], AF.Exp, bias=self.zero_col[P], scale=-0.5),
              reads=[ln_r, self.c_res], writes=[rs_r])
        return rs, rs_r

    def qk_post(self, Dh, raw_bank, gain_col, out_ap, out_res, rope=None, ss_bank=6, rot_bank=7):
        nc, fw = self.nc, self.fw
        P = slice(0, Dh)
        raw = self.pf[raw_bank]
        rr = self.pr[raw_bank]
        pend = self.__dict__.setdefault("pending_posts", [])
        while len(pend) >= POST_DEFER:
            pend.pop(0)()
        sq, sq_r, _ = self.sq_ring.next()
        fw.op(fw.act, lambda: nc.scalar.activation(sq[P], raw[P, :], AF.Square), reads=[rr], writes=[sq_r])
        qg, qg_r, _ = self.qg_ring.next()
        fw.op(fw.dve, lambda: nc.vector.tensor_scalar(qg[P], raw[P, :], gain_col, None, ALU.mult),
              reads=[rr, self.c_res], writes=[qg_r, rr])
        pend.append(lambda: self._post_b(Dh, sq, sq_r, qg, qg_r, out_ap, out_res, rope, ss_bank, rot_bank))

    def flush_post(self):
        pend = self.__dict__.setdefault("pending_posts", [])
        while pend:
            pend.pop(0)()

    def _post_b(self, Dh, sq, sq_r, qg, qg_r, out_ap, out_res, rope, ss_bank, rot_bank):
        nc, fw = self.nc, self.fw
        P = slice(0, Dh)
        fw.op(fw.pe, lambda: nc.tensor.matmul(self.pf[ss_bank][P, :], self.ones[P, 0:Dh], sq[P],
                                              start=True, stop=True),
              reads=[sq_r, self.c_res], writes=[self.pr[ss_bank]])
        if rope is not None:
            cos_ap, sin_ap, perm_ap, tab_res = rope
            fw.op(fw.pe, lambda: nc.tensor.matmul(self.pf[rot_bank][P, :], perm_ap, qg[P], start=True, stop=True),
                  reads=[qg_r, self.c_res], writes=[self.pr[rot_bank]])
        rs, rs_r = self.rstd_from_ss(Dh, ss_bank, Dh)
        if rope is None:
            fw.op(fw.dve, lambda: nc.vector.tensor_tensor(out_ap, qg[P], rs[P], ALU.mult),
                  reads=[qg_r, rs_r], writes=[out_res])
        else:
            t1, t1_r, _ = self.t1_ring.next()
            t2, t2_r, _ = self.t2_ring.next()
            fw.op(fw.dve, lambda: nc.vector.tensor_tensor(t1[P], qg[P], cos_ap, ALU.mult),
                  reads=[qg_r, tab_res], writes=[t1_r])
            fw.op(fw.dve, lambda: nc.vector.tensor_tensor(t2[P], self.pf[rot_bank][P, :], sin_ap, ALU.mult),
                  reads=[self.pr[rot_bank], tab_res], writes=[t2_r])
            fw.op(fw.dve, lambda: nc.vector.tensor_tensor(t1[P], t1[P], t2[P], ALU.add),
                  reads=[t1_r, t2_r], writes=[t1_r])
            fw.op(fw.dve, lambda: nc.vector.tensor_tensor(out_ap, t1[P], rs[P], ALU.mult),
                  reads=[t1_r, rs_r], writes=[out_res])

    def wo_phase(self, w_o, x_in, x_out):
        nc, fw = self.nc, self.fw
        with ExitStack() as st:
            OT = st.enter_context(nc.sbuf_tensor("OT_%d" % fw.uid(), [128, 16, S], BF16))
            R = Res()
            fw.dma(fw.sp, fw.dsem("otl"), [(OT[:, h, :], self.ot_scr[h]) for h in range(16)], writes=[R])
            self.proj_residual(OT, lambda ti: [R], 16, 16, w_o, x_in, x_out, 0, 4, "wo")

    def mla_mixer(self, x_in, x_out):
        nc, fw = self.nc, self.fw
        cols = self.cols
        with ExitStack() as st:
            cqT = st.enter_context(nc.sbuf_tensor("cqT", [128, 4, S], BF16))
            ckvT = st.enter_context(nc.sbuf_tensor("ckvT", [128, 4, S], BF16))
            krT = st.enter_context(nc.sbuf_tensor("krT", [64, S], BF16))
            cos64 = st.enter_context(nc.sbuf_tensor("cos64", [64, S], BF16))
            sin64 = st.enter_context(nc.sbuf_tensor("sin64", [64, S], BF16))
            tab_res = Res()
            cq_res = [Res() for _ in range(4)]
            ckv_res = [Res() for _ in range(4)]
            kr_res = [Res() for _ in range(4)]
            self.rope_tables(64, cols[0:64, 20:21], cols[0:64, 21:22], cos64, sin64, tab_res)
            if self.ck("tables"):
                return
            with ExitStack() as st1:
                hT = st1.enter_context(nc.sbuf_tensor("hT_m", [128, 16, S], BF16))
                hT_res = [[Res(), Res()] for _ in range(16)]
                wdn = st1.enter_context(nc.sbuf_tensor("wdn_m", [128, 16, 1088], BF16))
                wdn_r = Res()
                wv = self.w_down.rearrange("(k p) e -> p k e", p=128)
                fw.dma(fw.pool, fw.dsem("wdnm"), [(wdn[:, j * 4:(j + 1) * 4, :], wv[:, j * 4:(j + 1) * 4, :])
                                                  for j in range(4)], writes=[wdn_r])
                self.norm_to_hT(st1, x_in, self.mixer_norm[0:1, :], hT, hT_res, 0, 16)
                if self.ck("norm"):
                    return
                self.make_post(st1)
                cg_ring = Ring(self, st1, "cg", 2, [128, 4, 512], BF16)
                nb = 0
                for tb in range(4):
                    hr = [r for ti in range(tb * 4, tb * 4 + 4) for r in hT_res[ti]]
                    ts = slice(tb * 512, (tb + 1) * 512)
                    for grp in range(2):
                        cg, cg_r, _ = cg_ring.next()
                        ss_bank = 4 + grp
                        for c in range(4):
                            oc = grp * 4 + c
                            bank = nb % 3
                            nb += 1
                            pf = self.pf[bank]

                            def mm():
                                ins = None
                                for k in range(16):
                                    ins = nc.tensor.matmul(pf[:, :], wdn[:, k, oc * 128:(oc + 1) * 128], hT[:, k, ts],
                                                           start=(k == 0), stop=(k == 15))
                                return ins
                            fw.op(fw.pe, mm, reads=[wdn_r] + hr, writes=[self.pr[bank]])
                            sq, sq_r, _ = self.sq_ring.next()
                            fw.op(fw.act, lambda: nc.scalar.activation(sq[:], pf[:, :], AF.Square),
                                  reads=[self.pr[bank]], writes=[sq_r])
                            fw.op(fw.dve, lambda: nc.vector.tensor_scalar(cg[:, c, :], pf[:, :], cols[:, oc:oc + 1],
                                                                          None, ALU.mult),
                                  reads=[self.pr[bank], self.c_res], writes=[cg_r, self.pr[bank]])
                            fw.op(fw.pe, lambda: nc.tensor.matmul(self.pf[ss_bank][:, :], self.ones, sq[:],
                                                                  start=(c == 0), stop=(c == 3)),
                                  reads=[sq_r, self.c_res], writes=[self.pr[ss_bank]])
                        rs, rs_r = self.rstd_from_ss(128, ss_bank, 512)
                        dstT = cqT if grp == 0 else ckvT
                        dres = cq_res if grp == 0 else ckv_res
                        for c in range(4):
                            fw.op(fw.dve, lambda: nc.vector.tensor_tensor(dstT[:, c, ts], cg[:, c, :], rs[:],
                                                                          ALU.mult),
                                  reads=[cg_r, rs_r], writes=[dres[tb]])
                    bank = nb % 3
                    nb += 1
                    pf = self.pf[bank]

                    def mmr():
                        ins = None
                        for k in range(16):
                            ins = nc.tensor.matmul(pf[0:64, :], wdn[:, k, 1024:1088], hT[:, k, ts],
                                                   start=(k == 0), stop=(k == 15))
                        return ins
                    fw.op(fw.pe, mmr, reads=[wdn_r] + hr, writes=[self.pr[bank]])
                    self.qk_post(64, bank, cols[0:64, 11:12], krT[:, ts], kr_res[tb],
                                 rope=(cos64[:, ts], sin64[:, ts], self.perm64, tab_res))
                    self.flush_post()
                self.flush_post()
                fw.barrier()
            if self.ck("m1"):
                return
            scale = 1.0 / math.sqrt(192.0)
            with ExitStack() as st2:
                self.make_post(st2)
                wq_ring = Ring(self, st2, "wq", 2, [128, 4, 192], BF16, dma=True)
                wkv_ring = Ring(self, st2, "wkv", 2, [128, 4, 256], BF16, dma=True)
                qn_ring = Ring(self, st2, "qn", 2, [128, S], BF16)
                qr_ring = Ring(self, st2, "qr", 2, [64, S], BF16)
                kn_ring = Ring(self, st2, "kn", 2, [128, S], BF16)
                v_ring = Ring(self, st2, "vh", 2, [128, 16, 128], BF16)
                pt_ring = Ring(self, st2, "pt", LOOKAHEAD + 2, [128, 512], BF16)
                rl_ring = Ring(self, st2, "rl", 2, [128, 512], F32)
                oh_ring = Ring(self, st2, "oh", 2, [128, S], BF16, dma=True)
                wq_v = self.w_uq.rearrange("(k p) e -> p k e", p=128)
                wkv_v = self.w_ukv.rearrange("(k p) e -> p k e", p=128)
                allcq = cq_res
                allckv = ckv_res
                nst = 0
                for h in range(16):
                    wq, wq_r, wq_s = wq_ring.next()
                    fw.dma(fw.pool, wq_s, [(wq[:], wq_v[:, :, h * 192:(h + 1) * 192])], writes=[wq_r])
                    wkv, wkv_r, wkv_s = wkv_ring.next()
                    fw.dma(fw.pool, wkv_s, [(wkv[:], wkv_v[:, :, h * 256:(h + 1) * 256])], writes=[wkv_r])
                    qn, qn_r, _ = qn_ring.next()
                    qr, qr_r, _ = qr_ring.next()
                    kn, kn_r, _ = kn_ring.next()
                    vh, vh_r, _ = v_ring.next()
                    nb = 0
                    for tb in range(4):
                        ts = slice(tb * 512, (tb + 1) * 512)
                        for what in range(3):
                            bank = nb % 2
                            nb += 1
                            pf = self.pf[bank]
                            if what == 0:
                                Dh, wsl, src, srcr = 128, wq[:, :, 0:128], cqT, allcq[tb]
                            elif what == 1:
                                Dh, wsl, src, srcr = 64, wq[:, :, 128:192], cqT, allcq[tb]
                            else:
                                Dh, wsl, src, srcr = 128, wkv[:, :, 0:128], ckvT, allckv[tb]

                            def mm():
                                ins = None
                                for k in range(4):
                                    ins = nc.tensor.matmul(pf[0:Dh, :], wsl[:, k, :], src[:, k, ts],
                                                           start=(k == 0), stop=(k == 3))
                                return ins
                            fw.op(fw.pe, mm, reads=[wq_r if what < 2 else wkv_r, srcr], writes=[self.pr[bank]])
                            if what == 0:
                                self.qk_post(128, bank, cols[:, 8:9], qn[:, ts], qn_r)
                            elif what == 1:
                                self.qk_post(64, bank, cols[0:64, 9:10], qr[:, ts], qr_r,
                                             rope=(cos64[:, ts], sin64[:, ts], self.perm64, tab_res))
                            else:
                                self.qk_post(128, bank, cols[:, 10:11], kn[:, ts], kn_r)
                    self.flush_post()
                    for tg in range(4):
                        bank = nb % 2
                        nb += 1
                        pf = self.pf[bank]

                        def mmv():
                            ins = None
                            for j in range(4):
                                tt = tg * 4 + j
                                for k in range(4):
                                    ins = nc.tensor.matmul(pf[:, j * 128:(j + 1) * 128],
                                                           ckvT[:, k, tt * 128:(tt + 1) * 128], wkv[:, k, 128:256],
                                                           start=(j == 0 and k == 0), stop=(k == 3),
                                                           skip_group_check=True)
                            return ins
                        fw.op(fw.pe, mmv, reads=[wkv_r, allckv[tg]], writes=[self.pr[bank]])
                        fw.op(fw.act, lambda: nc.scalar.copy(vh[:, tg * 4:(tg + 1) * 4, :],
                                                             pf[:, :].rearrange("p (j d) -> p j d", j=4)),
                              reads=[self.pr[bank]], writes=[vh_r])
                    oh, oh_r, oh_s = oh_ring.next()
                    for qb in range(4):
                        qs0 = qb * 512
                        ob, lb = 4, 5
                        nk = 4 * qb + 4
                        pend_pv = []
                        for j in range(nk):
                            r = j - 4 * qb
                            c0 = 128 * r if r > 0 else 0
                            sb = ST_BANKS[nst % len(ST_BANKS)]
                            nst += 1
                            ks = slice(j * 128, (j + 1) * 128)
                            qs = slice(qs0 + c0, qs0 + 512)

                            def mms():
                                nc.tensor.matmul(self.pf[sb][:, c0:512], kn[:, ks], qn[:, qs], start=True, stop=False)
                                return nc.tensor.matmul(self.pf[sb][:, c0:512], krT[:, ks], qr[:, qs],
                                                        start=False, stop=True)
                            fw.op(fw.pe, mms, reads=[kn_r, qn_r, qr_r] + kr_res, writes=[self.pr[sb]])
                            pt, pt_r, _ = pt_ring.next()
                            fw.op(fw.act, lambda: nc.scalar.activation(pt[:, c0:512], self.pf[sb][:, c0:512], AF.Exp,
                                                                       bias=self.zero_col, scale=scale),
                                  reads=[self.pr[sb], self.c_res], writes=[pt_r])
                            if r >= 0:
                                fw.op(fw.dve, lambda: nc.vector.tensor_tensor(pt[:, c0:c0 + 128], pt[:, c0:c0 + 128],
                                                                              self.mask_cur, ALU.mult),
                                      reads=[pt_r, self.c_res], writes=[pt_r])

                            def mk_pv(j=j, c0=c0, pt=pt, pt_r=pt_r):
                                def mmo():
                                    nc.tensor.matmul(self.pf[ob][:, c0:512], vh[:, j, :], pt[:, c0:512],
                                                     start=(j == 0), stop=(j == nk - 1), skip_group_check=True)
                                    return nc.tensor.matmul(self.pf[lb][:, c0:512], self.ones, pt[:, c0:512],
                                                            start=(j == 0), stop=(j == nk - 1),
                                                            skip_group_check=True)
                                return lambda: fw.op(fw.pe, mmo, reads=[vh_r, pt_r, self.c_res],
                                                     writes=[self.pr[ob], self.pr[lb]])
                            pend_pv.append(mk_pv())
                            while len(pend_pv) > LOOKAHEAD:
                                pend_pv.pop(0)()
                        while pend_pv:
                            pend_pv.pop(0)()
                        rl, rl_r, _ = rl_ring.next()
                        fw.op(fw.dve, lambda: nc.vector.reciprocal(rl[:], self.pf[lb][:, :]), reads=[self.pr[lb]],
                              writes=[rl_r])
                        fw.op(fw.dve, lambda: nc.vector.tensor_tensor(oh[:, qs0:qs0 + 512], self.pf[ob][:, :], rl[:],
                                                                      ALU.mult),
                              reads=[self.pr[ob], rl_r], writes=[oh_r])
                    fw.dma(fw.sp, oh_s, [(self.ot_scr[h], oh[:])], reads=[oh_r])
                fw.barrier()
        if self.ck("m2"):
            return
        self.wo_phase(self.mla_w_o, x_in, x_out)

    def dil_mixer(self, x_in, x_out):
        nc, fw = self.nc, self.fw
        cols = self.cols
        DIL = (1, 4, 16)
        scale = 1.0 / math.sqrt(128.0)
        with ExitStack() as st:
            cos128 = st.enter_context(nc.sbuf_tensor("cos128", [128, S], BF16))
            sin128 = st.enter_context(nc.sbuf_tensor("sin128", [128, S], BF16))
            tab_res = Res()
            self.rope_tables(128, cols[:, 18:19], cols[:, 19:20], cos128, sin128, tab_res)
            hT = st.enter_context(nc.sbuf_tensor("hT_d", [128, 16, S], BF16))
            hT_res = [[Res(), Res()] for _ in range(16)]
            w_ring = Ring(self, st, "wqkv", 3, [128, 16, 384], BF16, dma=True)
            wv = self.w_qkv.rearrange("(k p) e -> p k e", p=128)

            def load_w(hh):
                wts = []
                for c in range(3):
                    wt, wt_r, wt_s = w_ring.next()
                    e0 = (hh * 9 + c * 3) * 128
                    fw.dma(fw.pool, wt_s, [(wt[:, j * 8:(j + 1) * 8, :], wv[:, j * 8:(j + 1) * 8, e0:e0 + 384])
                                           for j in range(2)], writes=[wt_r])
                    wts.append((wt, wt_r))
                return wts
            pre_wts = load_w(0)
            self.norm_to_hT(st, x_in, self.mixer_norm[1:2, :], hT, hT_res, 0, 16)
            all_h = [r for tr in hT_res for r in tr]
            self.make_post(st)
            qk = [[st.enter_context(nc.sbuf_tensor("qk%d%d" % (c, g), [128, S], BF16)) for g in range(3)]
                  for c in range(2)]
            qk_res = [[Res() for g in range(3)] for c in range(2)]
            vv = [st.enter_context(nc.sbuf_tensor("vv%d" % g, [128, 16, 128], BF16)) for g in range(3)]
            vv_res = [Res() for g in range(3)]
            accO = st.enter_context(nc.sbuf_tensor("accO", [128, S], F32))
            accL = st.enter_context(nc.sbuf_tensor("accL", [128, S], F32))
            acc_res = Res()
            pt_ring = Ring(self, st, "ptd", LOOKAHEAD + 2, [128, 256], BF16)
            oh_ring = Ring(self, st, "ohd", 2, [128, S], BF16, dma=True)
            nst = 0
            pend_final = None
            for hh in range(16):
                wts = pre_wts if hh == 0 else load_w(hh)
                nb = 0
                for c in range(2):
                    wt, wt_r = wts[c]
                    for g in range(3):
                        for tb in range(4):
                            ts = slice(tb * 512, (tb + 1) * 512)
                            hr = [r for ti in range(tb * 4, tb * 4 + 4) for r in hT_res[ti]]
                            bank = nb % 2
                            nb += 1
                            pf = self.pf[bank]

                            def mm():
                                ins = None
                                for k in range(16):
                                    ins = nc.tensor.matmul(pf[:, :], wt[:, k, g * 128:(g + 1) * 128], hT[:, k, ts],
                                                           start=(k == 0), stop=(k == 15))
                                return ins
                            fw.op(fw.pe, mm, reads=[wt_r] + hr, writes=[self.pr[bank]])
                            self.qk_post(128, bank, cols[:, 12 + c * 3 + g:13 + c * 3 + g], qk[c][g][:, ts],
                                         qk_res[c][g],
                                         rope=(cos128[:, ts], sin128[:, ts], self.perm128, tab_res))
                            if nb == 3 and pend_final is not None:
                                pend_final()
                                pend_final = None
                self.flush_post()
                wt, wt_r = wts[2]
                for g in range(3):
                    dil = DIL[g]
                    nlb = 16 // dil
                    for cg in range(4):
                        bank = nb % 2
                        nb += 1
                        pf = self.pf[bank]

                        def mmv():
                            ins = None
                            for j in range(4):
                                ci = cg * 4 + j
                                r, lbk = ci // nlb, ci % nlb
                                t_lo = lbk * 128 * dil + r
                                tsl = slice(t_lo, t_lo + 127 * dil + 1, dil)
                                for k in range(16):
                                    ins = nc.tensor.matmul(pf[:, j * 128:(j + 1) * 128], hT[:, k, tsl],
                                                           wt[:, k, g * 128:(g + 1) * 128],
                                                           start=(j == 0 and k == 0), stop=(k == 15),
                                                           skip_group_check=True)
                            return ins
                        fw.op(fw.pe, mmv, reads=[wt_r] + all_h, writes=[self.pr[bank]])
                        fw.op(fw.act, lambda: nc.scalar.copy(vv[g][:, cg * 4:(cg + 1) * 4, :],
                                                             pf[:, :].rearrange("p (j d) -> p j d", j=4)),
                              reads=[self.pr[bank]], writes=[vv_res[g]])
                for g in range(3):
                    dil = DIL[g]
                    nlb = 16 // dil
                    qT, kT = qk[0][g], qk[1][g]
                    started = {}

                    def pv(ci_q, pt_ap, kchunk, pt_r):
                        sbk = ci_q // 4
                        ob = 4 + (sbk % 2)
                        lb_ = 6 + (sbk % 2)
                        cc = (ci_q % 4) * 128
                        first = not started.get(sbk, False)
                        started[sbk] = True

                        def f():
                            nc.tensor.matmul(self.pf[ob][:, cc:cc + 128], vv[g][:, kchunk, :], pt_ap,
                                             start=first, stop=False, skip_group_check=True)
                            return nc.tensor.matmul(self.pf[lb_][:, cc:cc + 128], self.ones, pt_ap,
                                                    start=first, stop=False, skip_group_check=True)
                        fw.op(fw.pe, f, reads=[vv_res[g], pt_r, self.c_res], writes=[self.pr[ob], self.pr[lb_]])

                    def flush(sbk):
                        ob = 4 + (sbk % 2)
                        lb_ = 6 + (sbk % 2)
                        if dil == 1:
                            oa = accO[:, sbk * 512:(sbk + 1) * 512]
                            la = accL[:, sbk * 512:(sbk + 1) * 512]
                            po = self.pf[ob][:, :]
                            pl = self.pf[lb_][:, :]
                        elif dil == 4:
                            oa = accO[:, sbk:S:4]
                            la = accL[:, sbk:S:4]
                            po = self.pf[ob][:, :]
                            pl = self.pf[lb_][:, :]
                        else:
                            oa = accO[:].rearrange("p (a r) -> p r a", r=16)[:, sbk * 4:(sbk + 1) * 4, :]
                            la = accL[:].rearrange("p (a r) -> p r a", r=16)[:, sbk * 4:(sbk + 1) * 4, :]
                            po = self.pf[ob][:, :].rearrange("p (r a) -> p r a", r=4)
                            pl = self.pf[lb_][:, :].rearrange("p (r a) -> p r a", r=4)
                        if g == 0:
                            fw.op(fw.dve, lambda: nc.vector.tensor_copy(oa, po), reads=[self.pr[ob]],
                                  writes=[acc_res])
                            fw.op(fw.act, lambda: nc.scalar.copy(la, pl), reads=[self.pr[lb_]], writes=[acc_res])
                        else:
                            fw.op(fw.dve, lambda: nc.vector.tensor_tensor(oa, po, oa, ALU.add),
                                  reads=[self.pr[ob], acc_res], writes=[acc_res])
                            fw.op(fw.dve, lambda: nc.vector.tensor_tensor(la, pl, la, ALU.add),
                                  reads=[self.pr[lb_], acc_res], writes=[acc_res])

                    pend_tail = []
                    for ci in range(16):
                        r, lbk = ci // nlb, ci % nlb
                        has_next = lbk + 1 < nlb
                        nq = 256 if has_next else 128
                        t_lo = lbk * 128 * dil + r
                        ksl = slice(t_lo, t_lo + 127 * dil + 1, dil)
                        qsl = slice(t_lo, t_lo + (nq - 1) * dil + 1, dil)
                        sb = ST_BANKS[nst % len(ST_BANKS)]
                        nst += 1
                        fw.op(fw.pe, lambda: nc.tensor.matmul(self.pf[sb][:, 0:nq], kT[:, ksl], qT[:, qsl],
                                                              start=True, stop=True),
                              reads=[qk_res[0][g], qk_res[1][g]], writes=[self.pr[sb]])
                        pt, pt_r, _ = pt_ring.next()
                        fw.op(fw.act, lambda: nc.scalar.activation(pt[:, 0:nq], self.pf[sb][:, 0:nq], AF.Exp,
                                                                   bias=self.zero_col, scale=scale),
                              reads=[self.pr[sb], self.c_res], writes=[pt_r])
                        fw.op(fw.dve, lambda: nc.vector.tensor_tensor(pt[:, 0:nq], pt[:, 0:nq], self.mask2[:, 0:nq],
                                                                      ALU.mult),
                              reads=[pt_r, self.c_res], writes=[pt_r])
                        def mk_tail(ci=ci, pt=pt, pt_r=pt_r, has_next=has_next):
                            def tail():
                                pv(ci, pt[:, 0:128], ci, pt_r)
                                if has_next:
                                    pv(ci + 1, pt[:, 128:256], ci, pt_r)
                                if ci % 4 == 3:
                                    flush(ci // 4)
                            return tail
                        pend_tail.append(mk_tail())
                        while len(pend_tail) > LOOKAHEAD:
                            pend_tail.pop(0)()
                    while pend_tail:
                        pend_tail.pop(0)()
                def mk_final(hh=hh):
                    def fin():
                        oh, oh_r, oh_s = oh_ring.next()
                        fw.op(fw.dve, lambda: nc.vector.reciprocal(accL[:], accL[:]), reads=[acc_res],
                              writes=[acc_res])
                        fw.op(fw.dve, lambda: nc.vector.tensor_tensor(oh[:], accO[:], accL[:], ALU.mult),
                              reads=[acc_res], writes=[oh_r])
                        fw.dma(fw.sp, oh_s, [(self.ot_scr[hh], oh[:])], reads=[oh_r])
                    return fin
                pend_final = mk_final()
            if pend_final is not None:
                pend_final()
                pend_final = None
            fw.barrier()
        self.wo_phase(self.dil_w_o, x_in, x_out)


def _consts():
    bf = ml_dtypes.bfloat16
    cb = np.zeros((128, 768), np.float32)
    cb[:, 0:128] = np.eye(128)
    cb[:, 128:256] = 1.0
    dp = np.arange(128)
    perm = np.zeros((128, 128), np.float32)
    perm[(dp + 64) % 128, dp] = 1.0
    cb[:, 256:384] = perm
    dp = np.arange(64)
    perm64 = np.zeros((64, 64), np.float32)
    perm64[(dp + 32) % 64, dp] = 1.0
    cb[0:64, 384:448] = perm64
    a = np.arange(128)[:, None]
    b = np.arange(128)[None, :]
    cb[:, 512:640] = (a <= b)
    cb[:, 640:768] = (a >= b)
    return cb.astype(bf)


def _cols(inp):
    c = np.zeros((128, 32), np.float32)
    c[:, 0:4] = inp["mla_q_norm"][0].reshape(4, 128).T
    c[:, 4:8] = inp["mla_kv_norm"][0].reshape(4, 128).T
    c[:, 8] = inp["mla_q_gain"][0][:128]
    c[:64, 9] = inp["mla_q_gain"][0][128:]
    c[:, 10] = inp["mla_k_gain"][0][:128]
    c[:64, 11] = inp["mla_k_gain"][0][128:]
    c[:, 12:15] = inp["dil_q_gain"][0].T
    c[:, 15:18] = inp["dil_k_gain"][0].T
    p = np.arange(128)
    c[:, 18] = np.power(np.float32(10000.0), (-2.0 * (p % 64).astype(np.float32) / np.float32(128.0))).astype(np.float32)
    c[:, 19] = np.where(p < 64, -1.0, 1.0)
    p = np.arange(64)
    c[:64, 20] = np.power(np.float32(10000.0), (-2.0 * (p % 32).astype(np.float32) / np.float32(64.0))).astype(np.float32)
    c[:64, 21] = np.where(p < 32, -1.0, 1.0)
    return c


def make_in_maps(inp, cores):
    cb = _consts()
    cols = _cols(inp)
    f = lambda a: np.ascontiguousarray(np.asarray(a, dtype=np.float32))
    wqkv_r = np.ascontiguousarray(
        np.asarray(inp["dil_w_qkv"][0], np.float32).reshape(D, 9, 16, 128).transpose(0, 2, 1, 3).reshape(D, 18432))
    shared = {
        "mixer_norm": f(inp["mixer_norm"]), "ffn_norm": f(inp["ffn_norm"]),
        "mla_w_down": f(inp["mla_w_down"][0]), "mla_w_uq": f(inp["mla_w_uq"][0]),
        "mla_w_ukv": f(inp["mla_w_ukv"][0]), "mla_w_o": f(inp["mla_w_o"][0]),
        "dil_w_qkv_r": wqkv_r, "dil_w_o": f(inp["dil_w_o"][0]),
        "ffn_w_gate": f(inp["ffn_w_gate"]), "ffn_w_up": f(inp["ffn_w_up"]), "ffn_w_down": f(inp["ffn_w_down"]),
        "cols": cols, "cb": cb,
    }
    maps = []
    for b in cores:
        m = dict(shared)
        m["x"] = f(inp["x"][b])
        m["pos"] = np.ascontiguousarray(np.asarray(inp["positions"][b], np.int32).reshape(1, S))
        maps.append(m)
    return maps


def kernel(**inputs):
    prog = Prog()
    maps = make_in_maps(inputs, list(range(8)))
    res = run_bass_kernel_spmd(prog.nc, maps, core_ids=list(range(8)))
    return np.stack([np.asarray(r["out"], np.float32) for r in res.results], axis=0)
```
